# Optimizing a Trainium2 kernel written in Bass

```python
import jax, jax.numpy as jnp
from jax import lax
import numpy as np

D_MODEL = 4096
BATCH = 2
SEQ = 4096
DEPTH = 1
DEC_BATCH = 128
DEC_SEQ = 4
PAST_LEN = 8192
PAGE_SIZE = 128

D_MIX = D_MODEL
D_ATT = D_MIX // 2
D_CONV = D_MIX - D_ATT
HEAD_DIM = 128
N_HEADS = D_ATT // HEAD_DIM
N_KV = N_HEADS // 4
GQA_GROUP = N_HEADS // N_KV
KV_W = N_KV * HEAD_DIM
WINDOW = 128
BLOCK = WINDOW
CONV_W = 31
D_IN = D_ATT + 2 * KV_W + D_ATT + 2 * D_CONV + D_CONV
RMS_EPS = 1e-6
LN_EPS = 1e-5
NEG_INF = -1e30

kernel_name = 'hymba_swa_sink_conformer_step'


def _rms(x, g):
    xf = x.astype(jnp.float32)
    y = xf * lax.rsqrt(jnp.mean(xf * xf, -1, keepdims=True) + RMS_EPS)
    return (y * g.astype(jnp.float32)).astype(x.dtype)


def _alibi_slopes():
    return jnp.asarray(2.0 ** (-8.0 * np.arange(1, N_HEADS + 1) / N_HEADS), jnp.float32)


def _pre(x, c, w_ada, b_ada, norm_g, w_in, q_g, k_g):
    mod = jax.nn.silu(c) @ w_ada + b_ada
    shift, scale, gate = jnp.split(mod, 3, axis=-1)
    h = _rms(x, norm_g) * (1 + scale[:, None]) + shift[:, None]
    z = h @ w_in
    offs = [int(o) for o in np.cumsum([D_ATT, KV_W, KV_W, D_ATT, D_CONV, D_CONV])]
    q, k, v, ga, cu_a, cu_b, cg = jnp.split(z, offs, axis=-1)
    n, t = x.shape[:2]
    q = _rms(q.reshape(n, t, N_KV, GQA_GROUP, HEAD_DIM), q_g)
    k = _rms(k.reshape(n, t, N_KV, HEAD_DIM), k_g)
    v = v.reshape(n, t, N_KV, HEAD_DIM)
    u = cu_a * jax.nn.sigmoid(cu_b)
    return q, k, v, ga, u, cg, gate


def _sink_attention(q, k, v, dist, sinks):
    s = jnp.einsum('...qngd,...knd->...ngqk', q, k,
                   preferred_element_type=jnp.float32) * (HEAD_DIM ** -0.5)
    s = s - _alibi_slopes().reshape(N_KV, GQA_GROUP, 1, 1) * dist.astype(jnp.float32)
    s = jnp.where((dist >= 0) & (dist < WINDOW), s, NEG_INF)
    sink = sinks.astype(jnp.float32).reshape(N_KV, GQA_GROUP, 1, 1)
    m = jnp.maximum(jnp.max(s, -1, keepdims=True), sink)
    p = jnp.exp(s - m)
    p = p / (jnp.sum(p, -1, keepdims=True) + jnp.exp(sink - m))
    return jnp.einsum('...ngqk,...knd->...qngd', p.astype(v.dtype), v)


def _prompt_attention(q, k, v, sinks):
    n, t = k.shape[:2]
    nb = t // BLOCK
    qb = q.reshape(n, nb, BLOCK, N_KV, GQA_GROUP, HEAD_DIM)

    def band(a):
        ap = jnp.pad(a, ((0, 0), (BLOCK, 0), (0, 0), (0, 0)))
        ap = ap.reshape(n, nb + 1, BLOCK, N_KV, HEAD_DIM)
        return jnp.concatenate([ap[:, :-1], ap[:, 1:]], axis=2)

    qi = jnp.arange(BLOCK)[:, None]
    kj = jnp.arange(2 * BLOCK)[None, :]
    blk = jnp.arange(nb)[:, None, None]
    dist = jnp.where((blk > 0) | (kj >= BLOCK), BLOCK + qi - kj, WINDOW)
    o = _sink_attention(qb, band(k), band(v), dist[:, None, None], sinks)
    return o.reshape(n, t, D_ATT)


def _sample_attention(q, k_new, v_new, k_buf, v_buf, sinks):
    n, s_len = q.shape[:2]
    k_all = jnp.concatenate([k_buf.astype(k_new.dtype), k_new], axis=1)
    v_all = jnp.concatenate([v_buf.astype(v_new.dtype), v_new], axis=1)
    dist = jnp.arange(s_len)[:, None] + WINDOW - jnp.arange(WINDOW + s_len)[None, :]
    o = _sink_attention(q, k_all, v_all, dist, sinks)
    return o.reshape(n, s_len, D_ATT), k_all[:, -WINDOW:], v_all[:, -WINDOW:]


def _dwconv(u_ext, w, b):
    y = lax.conv_general_dilated(u_ext, w.astype(u_ext.dtype)[:, None, :], (1,), 'VALID',
                                 dimension_numbers=('NWC', 'WIO', 'NWC'),
                                 feature_group_count=D_CONV)
    return y + b


def _conv_tail(y, ln_g, ln_b, w_pw2, cg):
    yf = y.astype(jnp.float32)
    mu = jnp.mean(yf, -1, keepdims=True)
    var = jnp.mean(jnp.square(yf - mu), -1, keepdims=True)
    yn = ((yf - mu) * lax.rsqrt(var + LN_EPS) * ln_g.astype(jnp.float32)
          + ln_b.astype(jnp.float32)).astype(y.dtype)
    return (jax.nn.silu(yn) @ w_pw2) * jax.nn.silu(cg)


def _post(x, att, ga, conv, w_out, gate):
    o = jnp.concatenate([att * jax.nn.silu(ga), conv], axis=-1) @ w_out
    return x + gate[:, None] * o


def setup_inputs(seed: int = 0) -> dict:
    key = jax.random.key(seed)
    ks = jax.random.split(key, 22)
    f32 = jnp.float32

    def nrm(k, shape, s):
        return jax.random.normal(k, shape, f32) * s

    return {
        'x_prompt': nrm(ks[0], (BATCH, SEQ, D_MODEL), 1.0),
        'x_sample': nrm(ks[1], (DEC_BATCH, DEC_SEQ, D_MODEL), 1.0),
        'c_prompt': nrm(ks[2], (BATCH, D_MODEL), 1.0),
        'c_sample': nrm(ks[3], (DEC_BATCH, D_MODEL), 1.0),
        'cache_k_win': nrm(ks[4], (DEPTH, DEC_BATCH, WINDOW, N_KV, HEAD_DIM), 1.0),
        'cache_v_win': nrm(ks[5], (DEPTH, DEC_BATCH, WINDOW, N_KV, HEAD_DIM), 1.0),
        'state_conv': nrm(ks[6], (DEPTH, DEC_BATCH, CONV_W - 1, D_CONV), 0.5),
        'w_ada': nrm(ks[7], (DEPTH, D_MODEL, 3 * D_MODEL), 0.5 * D_MODEL ** -0.5),
        'b_ada': nrm(ks[8], (DEPTH, 3 * D_MODEL), 0.01),
        'norm_g': 1.0 + nrm(ks[9], (DEPTH, D_MODEL), 0.02),
        'w_in': nrm(ks[10], (DEPTH, D_MODEL, D_IN), D_MODEL ** -0.5),
        'q_norm_g': 1.0 + nrm(ks[11], (DEPTH, HEAD_DIM), 0.02),
        'k_norm_g': 1.0 + nrm(ks[12], (DEPTH, HEAD_DIM), 0.02),
        'sinks': nrm(ks[13], (DEPTH, N_HEADS), 0.5),
        'conv_w': nrm(ks[14], (DEPTH, CONV_W, D_CONV), CONV_W ** -0.5),
        'conv_b': nrm(ks[15], (DEPTH, D_CONV), 0.01),
        'ln_g': 1.0 + nrm(ks[16], (DEPTH, D_CONV), 0.02),
        'ln_b': nrm(ks[17], (DEPTH, D_CONV), 0.01),
        'w_pw2': nrm(ks[18], (DEPTH, D_CONV, D_CONV), D_CONV ** -0.5),
        'w_out': nrm(ks[19], (DEPTH, D_MIX, D_MODEL), D_MIX ** -0.5),
    }


def reference(x_prompt, x_sample, c_prompt, c_sample, cache_k_win, cache_v_win, state_conv,
              w_ada, b_ada, norm_g, w_in, q_norm_g, k_norm_g, sinks, conv_w, conv_b,
              ln_g, ln_b, w_pw2, w_out):
    hp, hs = x_prompt, x_sample
    kp_l, vp_l, cp_l, ks_l, vs_l, cs_l = [], [], [], [], [], []
    for l in range(DEPTH):
        q, k, v, ga, u, cg, g = _pre(hp, c_prompt, w_ada[l], b_ada[l], norm_g[l], w_in[l],
                                     q_norm_g[l], k_norm_g[l])
        att = _prompt_attention(q, k, v, sinks[l])
        u_ext = jnp.pad(u, ((0, 0), (CONV_W - 1, 0), (0, 0)))
        conv = _conv_tail(_dwconv(u_ext, conv_w[l], conv_b[l]), ln_g[l], ln_b[l], w_pw2[l], cg)
        kp_l.append(k[:, -WINDOW:])
        vp_l.append(v[:, -WINDOW:])
        cp_l.append(u_ext[:, -(CONV_W - 1):])
        hp = _post(hp, att, ga, conv, w_out[l], g)

        q, k, v, ga, u, cg, g = _pre(hs, c_sample, w_ada[l], b_ada[l], norm_g[l], w_in[l],
                                     q_norm_g[l], k_norm_g[l])
        att, k_buf, v_buf = _sample_attention(q, k, v, cache_k_win[l], cache_v_win[l], sinks[l])
        u_ext = jnp.concatenate([state_conv[l].astype(u.dtype), u], axis=1)
        conv = _conv_tail(_dwconv(u_ext, conv_w[l], conv_b[l]), ln_g[l], ln_b[l], w_pw2[l], cg)
        ks_l.append(k_buf)
        vs_l.append(v_buf)
        cs_l.append(u_ext[:, -(CONV_W - 1):])
        hs = _post(hs, att, ga, conv, w_out[l], g)

    return (hp, hs, jnp.stack(kp_l), jnp.stack(vp_l), jnp.stack(cp_l),
            jnp.stack(ks_l), jnp.stack(vs_l), jnp.stack(cs_l))
```

```python
import numpy as np
import concourse.bass as bass
import concourse.mybir as mybir
from concourse.bass_utils import run_bass_kernel_spmd

F32 = mybir.dt.float32
BF16 = mybir.dt.bfloat16
U8 = mybir.dt.uint8
AF = mybir.ActivationFunctionType
ALU = mybir.AluOpType

WINDOW = 128
CONV_W = 31
NSB = 16
NS = 64
RMS_EPS = 1e-6
LN_EPS = 1e-5
NEG = -30000.0
ISD = 128 ** -0.5


class Cfg:
    def __init__(self, D=4096, HQ=16, DC=2048, NT=8, n_cores=8, batch=2, seq=4096):
        self.D, self.HQ, self.DC, self.NT = D, HQ, DC, NT
        self.n_cores, self.batch, self.seq = n_cores, batch, seq
        self.NKC = D // 128
        self.HKV = HQ // 4
        self.NCC = DC // 128
        self.DATT = HQ * 128
        self.KVW = self.HKV * 128
        self.DIN = self.DATT + 2 * self.KVW + self.DATT + 3 * DC
        self.DMIX = self.DATT + DC
        self.NMC = self.DMIX // 128
        self.TT = 128 + NT * 128 + NS
        self.TQ = self.TT - 128
        self.oq = 0
        self.ok = self.DATT
        self.ov = self.ok + self.KVW
        self.oga = self.ov + self.KVW
        self.oca = self.oga + self.DATT
        self.ocb = self.oca + DC
        self.ocg = self.ocb + DC
        self.OW = 512 if D >= 512 else D


class Tok:
    __slots__ = ("sem", "val")

    def __init__(self, sem, val):
        self.sem, self.val = sem, val


class Arena:
    def __init__(self, pool, lo, hi):
        self.pool, self.lo, self.hi, self.cur = pool, lo, hi, lo
        self.peak = lo

    def alloc(self, free_shape, dtype, parts=128):
        esz = {F32: 4, BF16: 2, U8: 1}[dtype]
        n = int(np.prod(free_shape))
        nbytes = n * esz
        start = (self.cur + 63) // 64 * 64
        assert start + nbytes <= self.hi, f"arena overflow need {start + nbytes} hi {self.hi}"
        self.cur = start + nbytes
        self.peak = max(self.peak, self.cur)
        v = self.pool[0:parts, start:start + nbytes].bitcast(dtype)
        if len(free_shape) == 2:
            v = v.rearrange("p (a b) -> p a b", a=free_shape[0])
        elif len(free_shape) == 3:
            v = v.rearrange("p (a b c) -> p a b c", a=free_shape[0], b=free_shape[1])
        return v

    def mark(self):
        return self.cur

    def reset(self, m):
        self.cur = m


class K:
    def __init__(self, nc, stack):
        self.nc = nc
        self.q = {e: [] for e in ("pe", "act", "dve", "pool", "sp")}
        self.sem = {}
        self.cnt = {}
        for e in ("pe", "act", "dve"):
            self.sem[e] = stack.enter_context(nc.semaphore("s_" + e))
            self.cnt[e] = 0
        self.stack = stack
        self.nsem = 0

    def newsem(self, name):
        self.nsem += 1
        return self.stack.enter_context(self.nc.semaphore(f"{name}_{self.nsem}"))

    def op(self, eng, fn, deps=(), sig=True):
        deps = [d for d in deps if d is not None]
        tok = None
        if sig:
            self.cnt[eng] += 1
            tok = Tok(self.sem[eng], self.cnt[eng])
        self.q[eng].append((fn, deps, tok, 1))
        return tok

    def dma(self, queue, fn, sem, val, deps=()):
        deps = [d for d in deps if d is not None]
        tok = Tok(sem, val)
        self.q[queue].append((fn, deps, tok, 16))
        return tok

    def emit(self, eng, e):
        seen = {}
        for fn, deps, tok, inc in self.q[eng]:
            for d in deps:
                key = id(d.sem)
                if seen.get(key, -1) >= d.val:
                    continue
                seen[key] = d.val
                e.wait_ge(d.sem, d.val)
            ins = fn(e)
            if tok is not None:
                ins.then_inc(tok.sem, inc)

    def final_wait(self, eng, toks):
        self.q[eng].append((None, toks, None, 0))


class DmaSlot:
    def __init__(self, k, name):
        self.sem = k.newsem(name)
        self.n = 0

    def next(self):
        self.n += 1
        return self.sem, 16 * self.n


def build(cfg, stop=99, sub=99):
    nc = bass.Bass("TRN2", target_bir_lowering=False)
    from contextlib import ExitStack
    c = cfg
    D, NKC, HQ, HKV, DC, NCC, NT, TT, TQ, NMC = c.D, c.NKC, c.HQ, c.HKV, c.DC, c.NCC, c.NT, c.TT, c.TQ, c.NMC
    NTILE = NT + 2

    def din(name, shape, dt=F32):
        return nc.dram_tensor(name, list(shape), dt, kind="ExternalInput").ap()

    def dout(name, shape):
        return nc.dram_tensor(name, list(shape), F32, kind="ExternalOutput").ap()

    x = din("x", [TT, D])
    csel = din("csel", [128, D])
    w_ada = din("w_ada", [D, 3 * D])
    b_ada = din("b_ada", [3 * D])
    norm_g = din("norm_g", [D])
    w_in = din("w_in", [D, c.DIN])
    q_g = din("q_g", [128])
    k_g = din("k_g", [128])
    sinks = din("sinks", [HQ])
    conv_w = din("conv_w", [CONV_W, DC])
    conv_b = din("conv_b", [DC])
    ln_g = din("ln_g", [DC])
    ln_b = din("ln_b", [DC])
    w_pw2 = din("w_pw2", [DC, DC])
    w_out = din("w_out", [c.DMIX, D])
    ck = din("ck", [NSB, WINDOW, HKV, 128])
    cv = din("cv", [NSB, WINDOW, HKV, 128])
    st = din("st", [NSB, CONV_W - 1, DC])
    flag = din("flag", [128, 1])
    identf = din("identf", [128, 128])
    biasp = din("biasp", [HKV, 128, 2, 4, 128])
    biasp0 = din("biasp0", [HKV, 128, 4, 128])
    biasc = din("biasc", [HKV, 128, 4 * NS])
    biasn = din("biasn", [HKV, NS, 4 * NS])

    y_p = dout("y_p", [NT * 128, D])
    y_s = dout("y_s", [NS, D])
    k_p = dout("k_p", [128, c.KVW])
    v_p = dout("v_p", [128, c.KVW])
    conv_p = dout("conv_p", [CONV_W - 1, DC])
    k_so = dout("k_so", [NSB, WINDOW - 4, c.KVW])
    v_so = dout("v_so", [NSB, WINDOW - 4, c.KVW])
    conv_so = dout("conv_so", [NSB, CONV_W - 5, DC])
    k_sn = dout("k_sn", [NS, c.KVW])
    v_sn = dout("v_sn", [NS, c.KVW])
    conv_sn = dout("conv_sn", [NS, DC])
    gscr = nc.dram_tensor("gscr", [128, D], F32, kind="Internal").ap()

    stack = ExitStack()
    POOLB = 212800
    pool_t = stack.enter_context(nc.sbuf_tensor("pool", [128, POOLB], U8))
    psum_t = stack.enter_context(nc.psum_tensor("ps", [128, 8, 512], F32))
    k = K(nc, stack)
    ps = psum_t[:]
    ps_bf = ps.bitcast(BF16)

    A = Arena(pool_t, 0, POOLB)
    HT = A.alloc([NKC, TT], BF16)
    MIX = A.alloc([NMC, TQ], BF16)
    NSLOT = 2
    WR = [A.alloc([NKC, 128], BF16) for _ in range(NSLOT)]
    IDF = A.alloc([128], F32)
    IDB = A.alloc([128], BF16)
    ONESB = A.alloc([128], BF16)
    ONESF = A.alloc([128], F32)
    BT = A.alloc([3 * NKC], F32)
    NV = NKC + 3 * NCC + 2
    VEC = A.alloc([NV], F32)
    NG = VEC[:, 0:NKC]
    CB = VEC[:, NKC:NKC + NCC]
    LG = VEC[:, NKC + NCC:NKC + 2 * NCC]
    LB = VEC[:, NKC + 2 * NCC:NKC + 3 * NCC]
    QG = VEC[:, NKC + 3 * NCC:NKC + 3 * NCC + 1]
    KG = VEC[:, NKC + 3 * NCC + 1:NKC + 3 * NCC + 2]
    CW = A.alloc([NCC, CONV_W], F32)
    ESK = A.alloc([HQ], F32)
    FLG = A.alloc([1], F32)
    QS = A.alloc([HQ, NS], BF16)
    KSN = A.alloc([HKV, NS], BF16)
    VSN = A.alloc([HKV, 128], BF16, parts=NS)
    SMALL = A.alloc([16], F32)
    base_mark = A.mark()

    wslots = [DmaSlot(k, "w") for _ in range(NSLOT)]
    wfree = [None] * NSLOT
    wstate = {"i": 0}
    out_toks = []

    def chunks(t0, t1):
        r, s = [], t0
        while s < t1:
            n = min(512, t1 - s)
            r.append((s, n))
            s += n
        return r

    bank_free = [None] * 8
    setstate = {"i": 0}

    def next_set():
        s = setstate["i"]
        setstate["i"] ^= 1
        return [0, 1, 2] if s == 0 else [3, 4, 5]

    miscstate = {"i": 0}

    def next_misc():
        b = 6 + miscstate["i"]
        miscstate["i"] ^= 1
        return b

    def free_deps(banks):
        return [bank_free[b] for b in banks]

    def set_free(banks, tok):
        for b in banks:
            bank_free[b] = tok

    pending = []

    tickers = []

    def flush_pending():
        todo = list(pending)
        del pending[:]
        for f in todo:
            f()
        for t in tickers:
            t()

    def issue_weight(desc):
        src, nkc = desc
        s = wstate["i"] % NSLOT
        wstate["i"] += 1
        sem, val = wslots[s].next()
        tok = k.dma("pool", lambda e, s=s, src=src, nkc=nkc: e.dma_start(out=WR[s][:, 0:nkc, :], in_=src),
                    sem, val, deps=[wfree[s]])
        return s, tok

    def run_blocks(all_blocks):
        LOOK = NSLOT - 1
        ring = [b for b in all_blocks if "own" not in b]
        own = [b for b in all_blocks if "own" in b]
        issued = {}
        for j in range(min(LOOK, len(ring))):
            issued[id(ring[j])] = issue_weight(ring[j]["w"])
        ring_pos = {id(b): i for i, b in enumerate(ring)}
        own_pos = {id(b): i for i, b in enumerate(own)}

        def issue_own(b):
            buf, dslot, st_ = b["own"]
            src, nkc_ = b["w"]
            sem, val = dslot.next()
            tok_ = k.dma("pool", lambda e: e.dma_start(out=buf[:, 0:nkc_, :], in_=src), sem, val,
                         deps=[st_.get("free")] + list(b.get("own_deps", [])))
            issued[id(b)] = (buf, tok_)
        if own:
            issue_own(own[0])
        for b in all_blocks:
            if "own" in b:
                wbuf, wtok = issued[id(b)]
                slot = None
            else:
                j = ring_pos[id(b)]
                if j + LOOK < len(ring):
                    issued[id(ring[j + LOOK])] = issue_weight(ring[j + LOOK]["w"])
                slot, wtok = issued[id(b)]
                wbuf = WR[slot]
            nkc = b["w"][1]
            banks = next_set()
            ch = chunks(b["t0"], b["t1"])
            deps = [wtok] + free_deps(banks[:len(ch)]) + list(b.get("deps", []))
            last = None
            for kc in range(nkc):
                for ci, (c0, n) in enumerate(ch):
                    islast = (kc == nkc - 1 and ci == len(ch) - 1)
                    last = k.op("pe", lambda e, kc=kc, ci=ci, c0=c0, n=n, wbuf=wbuf, banks=banks, b=b, nkc=nkc:
                                e.matmul(ps[:, banks[ci], 0:n], wbuf[:, kc, :], b["mov"](kc, c0, n),
                                         start=(kc == 0), stop=(kc == nkc - 1)),
                                deps=deps if (kc == 0 and ci == 0) else (), sig=islast)
                if nkc >= 9 and "own" not in b and kc in (nkc // 3 - 1, (2 * nkc) // 3 - 1):
                    flush_pending()
                elif nkc == 8 and "own" not in b and kc == 3:
                    flush_pending()
            if "own" in b:
                b["own"][2]["free"] = last
                b["own"][2]["last_pe"] = last
                i_ = own_pos[id(b)]
                if i_ + 1 < len(own):
                    issue_own(own[i_ + 1])
            else:
                wfree[slot] = last
            flush_pending()
            ftok = b["handler"](banks, ch, last)
            set_free(banks[:len(ch)], ftok)
        flush_pending()
        flush_pending()

    def psv(banks, ch):
        return [(ps[:, banks[i], 0:n], c0, n) for i, (c0, n) in enumerate(ch)]

    def act_chunks(banks, ch, tok, outfn, func, **kw):
        t = None
        for (p, c0, n) in psv(banks, ch):
            t = k.op("act", lambda e, p=p, c0=c0, n=n: e.activation(outfn(c0, n), p, func, **kw), deps=[tok])
        return t

    def dve_copy_chunks(banks, ch, tok, outfn):
        t = None
        for (p, c0, n) in psv(banks, ch):
            t = k.op("dve", lambda e, p=p, c0=c0, n=n: e.tensor_copy(outfn(c0, n), p), deps=[tok])
        return t

    gscr_toks = []

    def finish():
        k.final_wait("sp", [t for t in out_toks if t is not None] + gscr_toks)

        with nc.Block() as block:
            def mk(engname):
                def f(e):
                    items = k.q[engname]
                    seen = {}
                    for fn, deps, tok, inc in items:
                        for d in deps:
                            key = id(d.sem)
                            if seen.get(key, -1) >= d.val:
                                continue
                            seen[key] = d.val
                            e.wait_ge(d.sem, d.val)
                        if fn is None:
                            continue
                        ins = fn(e)
                        if tok is not None:
                            ins.then_inc(tok.sem, inc)
                return f
            block.tensor(mk("pe"))
            block.scalar(mk("act"))
            block.vector(mk("dve"))
            block.gpsimd(mk("pool"))
            block.sync(mk("sp"))

        stack.close()
        return nc

    css = [DmaSlot(k, f"const{i}") for i in range(6)]
    cst_i = {"i": 0, "last": [None] * 6}
    ctoks = []

    def cload(out_ap, in_ap, slow=False, deps=()):
        i = cst_i["i"] % 6
        cst_i["i"] += 1
        sem, val = css[i].next()
        t = k.dma("sp", lambda e: e.dma_start(out=out_ap, in_=in_ap, allow_slow_non_contiguous=slow), sem, val,
                  deps=[cst_i["last"][i]] + list(deps))
        cst_i["last"][i] = t
        ctoks.append(t)
        return t

    T_IDF = cload(IDF, identf)
    t_esk = cload(ESK, sinks.partition_broadcast(128))

    def const_loads_late(RW1, RW2, CWS, t_cwz):
        toks = []
        toks.append(cload(RW1[0:3 * NKC, :], b_ada.rearrange("(a p) -> a p", p=128)))
        r = 0
        for (src, n_) in ((norm_g, NKC), (conv_b, NCC), (ln_g, NCC), (ln_b, NCC), (q_g, 1), (k_g, 1)):
            toks.append(cload(RW2[r:r + n_, :], src.rearrange("(a p) -> a p", p=128)))
            r += n_
        toks.append(cload(CWS, conv_w, deps=[t_cwz]))
        toks.append(cload(FLG, flag))
        return toks
    osem = DmaSlot(k, "oldrows")
    for (dst, src) in ((k_so, ck.rearrange("b j n d -> b j (n d)")[:, 4:WINDOW, :]),
                       (v_so, cv.rearrange("b j n d -> b j (n d)")[:, 4:WINDOW, :]),
                       (conv_so, st[:, 4:CONV_W - 1, :])):
        sem, val = osem.next()
        out_toks.append(k.dma("sp", lambda e, dst=dst, src=src: e.dma_start(out=dst, in_=src), sem, val,
                              deps=[out_toks[-1]] if out_toks else []))

    T_SMALL = k.op("dve", lambda e: e.memset(SMALL, 0.0))
    T_SMALL = k.op("dve", lambda e: e.memset(SMALL[:, 9:10], RMS_EPS), deps=[T_SMALL])
    t_c1 = k.op("dve", lambda e: e.tensor_copy(IDB, IDF), deps=[T_IDF])
    t_c2 = k.op("dve", lambda e: e.memset(ONESB, 1.0))
    t_c3 = k.op("dve", lambda e: e.memset(ONESF, 1.0))
    t_c4 = k.op("act", lambda e: e.activation(ESK, ESK, AF.Exp), deps=[t_esk])

    jstate = {"last": None}

    def join(deps):
        t = k.op("act", lambda e: e.activation(SMALL[:, 8:9], SMALL[:, 8:9], AF.Copy),
                 deps=list(deps) + [T_SMALL, jstate["last"]])
        jstate["last"] = t
        return t

    if stop <= 0:
        return finish()
    NSL = 66
    CTT = A.alloc([NKC, NSL], BF16)
    GTB = [A.alloc([NSL], F32) for _ in range(2)]
    GOB = [A.alloc([128], F32) for _ in range(2)]
    m0 = A.mark()
    GALL = A.alloc([NKC, NSL], F32)
    SHALL = A.alloc([NKC, NSL], F32)
    m1 = A.mark()
    mix_lo = NKC * TT * 2
    mix_lo = (mix_lo + 63) // 64 * 64
    MOV = Arena(pool_t, mix_lo, mix_lo + NMC * TQ * 2)
    SLB = 4 if NKC % 4 == 0 and D >= 512 else 1
    big01 = MOV if (NMC * TQ * 2) >= max(2 * NKC * SLB * 128 * 2, 6 * D * 2) + 4096 else A
    if big01 is MOV and NKC * SLB * 128 * 2 >= 6 * D + 128:
        w0 = big01.alloc([NKC, SLB * 128], BF16)
        mk_ = big01.mark()
        CIN = big01.alloc([D], F32)
        CSL = big01.alloc([D], BF16)
        big01.reset(mk_)
        WSL = [w0, big01.alloc([NKC, SLB * 128], BF16)]
        wsl1_alias = True
    else:
        CIN = A.alloc([D], F32)
        CSL = A.alloc([D], BF16)
        WSL = [big01.alloc([NKC, SLB * 128], BF16) for _ in range(2)]
        wsl1_alias = False
    cin_s = DmaSlot(k, "cin")
    sem, val = cin_s.next()
    t_cin = k.dma("sp", lambda e: e.dma_start(out=CIN, in_=csel), sem, val)
    wsl = [DmaSlot(k, "wsl0"), DmaSlot(k, "wsl1")]
    wav = w_ada.rearrange("(a p) n -> p a n", p=128)
    NSLB = 2 * NKC // SLB
    wsl_tok = {}
    wsl_free = [None, None]

    def issue_wsl(j):
        sl = j % 2
        sem, val = wsl[sl].next()
        wsl_tok[j] = k.dma("pool", lambda e: e.dma_start(out=WSL[sl], in_=wav[:, :, j * SLB * 128:(j + 1) * SLB * 128]),
                           sem, val, deps=[wsl_free[sl]])
    issue_wsl(0)
    RW1 = A.alloc([128], F32)
    RW2 = A.alloc([128], F32)
    CWS = A.alloc([DC], F32, parts=32)
    t_cwz = k.op("dve", lambda e: e.memset(CWS, 0.0))
    ltoks = const_loads_late(RW1, RW2, CWS[0:CONV_W, :], t_cwz)
    bank = next_misc()
    k.op("pe", lambda e, bank=bank: e.transpose(ps[:, bank, 0:3 * NKC], RW1[0:3 * NKC, :], IDF[0:3 * NKC, 0:3 * NKC]),
         deps=ltoks + [T_IDF, bank_free[bank]], sig=False)
    tp_ = k.op("pe", lambda e, bank=bank: e.transpose(ps[:, bank, 128:128 + NV], RW2[0:NV, :], IDF[0:NV, 0:NV]))
    k.op("dve", lambda e, bank=bank: e.tensor_copy(BT, ps[:, bank, 0:3 * NKC]), deps=[tp_], sig=False)
    t_v = k.op("dve", lambda e, bank=bank: e.tensor_copy(VEC, ps[:, bank, 128:128 + NV]))
    bank_free[bank] = t_v
    bank = next_misc()
    tp_ = None
    for cc in range(NCC):
        tp_ = k.op("pe", lambda e, cc=cc, bank=bank: e.transpose(ps[:, bank, cc * 32:(cc + 1) * 32],
                                                                CWS[:, cc * 128:(cc + 1) * 128], IDF[0:32, 0:32]),
                   deps=ltoks + [T_IDF, bank_free[bank]] if cc == 0 else (), sig=(cc == NCC - 1))
    t_cw = k.op("dve", lambda e, bank=bank: e.tensor_copy(CW, ps[:, bank, 0:NCC * 32].rearrange("p (a b) -> p a b", b=32)[:, :, 0:CONV_W]),
                deps=[tp_])
    bank_free[bank] = t_cw
    t_c5 = k.op("dve", lambda e: e.tensor_scalar(BT[:, NKC:2 * NKC], BT[:, NKC:2 * NKC], 1.0, None, ALU.add), deps=[t_v])
    CONST_TOKS = [t_c1, t_c2, t_c3, t_c4, t_c5, t_cw, t_v, T_SMALL] + ltoks
    t_silu = k.op("act", lambda e: e.activation(CSL, CIN, AF.Silu), deps=[t_cin])
    t_ct = None
    for g0 in range(0, NKC, 8):
        bank = next_misc()
        g1 = min(NKC, g0 + 8)
        lastpe = None
        for kc in range(g0, g1):
            lastpe = k.op("pe", lambda e, kc=kc, bank=bank, g0=g0: e.transpose(
                ps_bf[:, bank, (kc - g0) * 128:(kc - g0 + 1) * 128], CSL[:, kc * 128:(kc + 1) * 128], IDB),
                deps=[t_silu, t_c1, bank_free[bank]] if kc == g0 else (), sig=(kc == g1 - 1))
        t_ct = k.op("dve", lambda e, bank=bank, g0=g0, g1=g1: e.tensor_copy(
            CTT[:, g0:g1, :], ps_bf[:, bank, 0:(g1 - g0) * 128].rearrange("p (a b) -> p a b", b=128)[:, :, 0:NSL]),
            deps=[lastpe])
        bank_free[bank] = t_ct

    if NSLB > 1:
        wsl_free[1] = t_ct
        issue_wsl(1)
    pb = {"i": 0}
    last_ev = None
    for j in range(NSLB):
        sl = j % 2
        lp = None
        for bi in range(SLB):
            q = j * SLB + bi
            part, kc_o = q // NKC, q % NKC
            bank = pb["i"] % 6
            pb["i"] += 1
            for kc in range(NKC):
                lp = k.op("pe", lambda e, kc=kc, sl=sl, bi=bi, bank=bank: e.matmul(
                    ps[:, bank, 0:NSL], WSL[sl][:, kc, bi * 128:(bi + 1) * 128], CTT[:, kc, :],
                    start=(kc == 0), stop=(kc == NKC - 1)),
                    deps=[wsl_tok[j], t_ct, bank_free[bank]] if kc == 0 else (), sig=(kc == NKC - 1))
            if part == 0:
                ev = k.op("dve", lambda e, kc_o=kc_o, bank=bank: e.tensor_scalar(
                    SHALL[:, kc_o, :], ps[:, bank, 0:NSL], BT[:, kc_o:kc_o + 1], None, ALU.add), deps=[lp] + CONST_TOKS)
            else:
                ev = k.op("dve", lambda e, kc_o=kc_o, bank=bank: e.tensor_scalar(
                    GALL[:, kc_o, :], ps[:, bank, 0:NSL], BT[:, NKC + kc_o:NKC + kc_o + 1], NG[:, kc_o:kc_o + 1],
                    ALU.add, ALU.mult), deps=[lp] + CONST_TOKS)
            bank_free[bank] = ev
            last_ev = ev
        wsl_free[sl] = lp
        if j + 2 < NSLB:
            issue_wsl(j + 2)

    gsl = [DmaSlot(k, "gscr0"), DmaSlot(k, "gscr1")]
    gstate = {"i": 0, "toks": [None, None], "dtoks": [None, None]}

    def h_gate(kc):
        def h(banks, ch, tok):
            i = gstate["i"] % 2
            gstate["i"] += 1
            t1 = k.op("act", lambda e: e.activation(GTB[i], ps[:, banks[0], 0:NSL], AF.Identity,
                                                    bias=BT[:, 2 * NKC + kc:2 * NKC + kc + 1]),
                      deps=[tok, gstate["toks"][i]] + CONST_TOKS)

            def later():
                bank = next_misc()
                t2 = k.op("pe", lambda e: e.transpose(ps[0:NSL, bank, 0:128], GTB[i], IDF),
                          deps=[t1, bank_free[bank]])
                gstate["toks"][i] = t2
                t3 = k.op("act", lambda e: e.activation(GOB[i][0:NSL, :], ps[0:NSL, bank, 0:128], AF.Copy),
                          deps=[t2, gstate["dtoks"][i]])
                bank_free[bank] = t3
                sem, val = gsl[i].next()
                t4 = k.dma("sp", lambda e: e.dma_start(out=gscr[0:NSL, kc * 128:(kc + 1) * 128], in_=GOB[i][0:NSL, :]),
                           sem, val, deps=[t3])
                gstate["dtoks"][i] = t4
                gscr_toks.append(t4)
            pending.append(later)
            return t1
        return h

    ngm = -(-(NKC * 128) // TQ)
    GW_OK = (HQ - ngm) >= 1
    gate_state = {"free": None}
    if GW_OK:
        gw_lo = mix_lo + (HQ - ngm) * TQ * 2
        GWB = pool_t[:, gw_lo:gw_lo + NKC * 128 * 2].bitcast(BF16).rearrange("p (a b) -> p a b", a=NKC)
        gw_slot = DmaSlot(k, "gatew")
    gate_blocks = []
    for kc in range(NKC):
        gb = dict(w=(wav[:, :, 2 * D + kc * 128:2 * D + (kc + 1) * 128], NKC),
                  mov=lambda kk, c0, n: CTT[:, kk, c0:c0 + n], t0=0, t1=NSL, deps=[t_ct], handler=h_gate(kc))
        if GW_OK:
            gb["own"] = (GWB, gw_slot, gate_state)
        gate_blocks.append(gb)
    T_MOD = last_ev
    PH0_DONE = [T_MOD, Tok(k.sem["pe"], k.cnt["pe"])]
    A.reset(m1)
    if big01 is MOV:
        MOV.reset(mix_lo)

    if stop <= 1:
        return finish()
    big1 = MOV if (NMC * TQ * 2) >= 6 * D * 2 + 4096 else A
    NXS = 3
    XS = [big1.alloc([D], F32) for _ in range(NXS)]
    XB = [big1.alloc([D], BF16) for _ in range(2)]
    TMPS = A.alloc([8, NS], F32)
    xsl = [DmaSlot(k, f"x{i}") for i in range(NXS)]
    xfree = [None] * NXS
    xbfree = [None, None]
    ht_tok = None
    grp = 8 if NKC >= 8 else NKC
    pbank = {"i": 0}
    for ti in range(NTILE):
        rows = 128 if ti < NTILE - 1 else NS
        r0 = ti * 128
        s = ti % 2
        sx = ti % NXS
        sem, val = xsl[sx].next()
        t_x = k.dma("sp", lambda e, sx=sx, r0=r0, rows=rows: e.dma_start(out=XS[sx][0:rows, :], in_=x[r0:r0 + rows, :]),
                    sem, val, deps=[xfree[sx]] + PH0_DONE)
        ssq = SMALL[:, 2 * sx:2 * sx + 1]
        rs = SMALL[:, 2 * sx + 1:2 * sx + 2]
        t_sq = k.op("act", lambda e, s=s, sx=sx, rows=rows, ssq=ssq: e.activation(XB[s][0:rows, :], XS[sx][0:rows, :], AF.Square,
                                                                                accum_out=ssq[0:rows, :]),
                    deps=[t_x, xbfree[s]] + PH0_DONE)
        t_a = k.op("dve", lambda e, rows=rows, ssq=ssq, rs=rs: e.tensor_scalar(rs[0:rows, :], ssq[0:rows, :], 1.0 / D, RMS_EPS,
                                                                            ALU.mult, ALU.add), deps=[t_sq])
        t_b = k.op("act", lambda e, rows=rows, rs=rs: e.activation(rs[0:rows, :], rs[0:rows, :], AF.Ln), deps=[t_a])
        t_c = k.op("act", lambda e, rows=rows, rs=rs: e.activation(rs[0:rows, :], rs[0:rows, :], AF.Exp, scale=-0.5),
                   deps=[t_b])
        t_xb = k.op("dve", lambda e, s=s, sx=sx, rows=rows, rs=rs: e.tensor_scalar(XB[s][0:rows, :], XS[sx][0:rows, :],
                                                                                 rs[0:rows, :], None, ALU.mult),
                    deps=[t_c, t_sq])
        xfree[sx] = t_xb
        lastev = None
        if sub <= 1:
            continue
        for g0 in range(0, NKC, grp):
            bank = pbank["i"] % 8
            pbank["i"] += 1
            lastpe = None
            for kc in range(g0, g0 + grp):
                lastpe = k.op("pe", lambda e, kc=kc, bank=bank, g0=g0, s=s, rows=rows: e.transpose(
                    ps_bf[:, bank, (kc - g0) * 128:(kc - g0) * 128 + rows], XB[s][0:rows, kc * 128:(kc + 1) * 128],
                    IDB[0:rows, 0:rows]),
                    deps=[t_xb, bank_free[bank], t_c1] if kc == g0 else (), sig=(kc == g0 + grp - 1))
            if sub <= 2:
                bank_free[bank] = lastpe
                continue
            if ti < NTILE - 1:
                ev = None
                ev2 = None
                for kc in range(g0, g0 + grp):
                    src = ps_bf[:, bank, (kc - g0) * 128:(kc - g0 + 1) * 128]
                    dst = HT[:, kc, r0:r0 + 128]
                    if ((g0 // grp + ti) % 2 == 0 and sub != 4) or sub == 3:
                        ev = k.op("dve", lambda e, src=src, dst=dst, kc=kc: e.tensor_scalar(
                            dst, src, GALL[:, kc, 64:65], SHALL[:, kc, 64:65], ALU.mult, ALU.add), deps=[lastpe, T_MOD])
                    else:
                        ev2 = k.op("act", lambda e, src=src, dst=dst, kc=kc: e.activation(
                            dst, src, AF.Identity, bias=SHALL[:, kc, 64:65], scale=GALL[:, kc, 64:65]),
                            deps=[lastpe, T_MOD])
                evj = join([t for t in (ev, ev2) if t is not None])
                bank_free[bank] = evj
                lastev = evj
            elif sub in (3, 4):
                bank_free[bank] = lastpe
            else:
                pv = ps_bf[:, bank, 0:grp * 128].rearrange("p (a b) -> p a b", b=128)[:, :, 0:NS]
                tm = TMPS[:, 0:grp, :]
                e1 = k.op("dve", lambda e, pv=pv, tm=tm, g0=g0: e.tensor_tensor(tm, pv, GALL[:, g0:g0 + grp, 0:NS], ALU.mult),
                          deps=[lastpe, T_MOD, lastev])
                e2 = k.op("dve", lambda e, tm=tm, g0=g0, r0=r0: e.tensor_tensor(HT[:, g0:g0 + grp, r0:r0 + NS], tm,
                                                                             SHALL[:, g0:g0 + grp, 0:NS], ALU.add),
                          deps=[e1])
                bank_free[bank] = e2
                lastev = e2
        xbfree[s] = lastpe
        ht_tok = lastev
    HT_DONE = [Tok(k.sem["dve"], k.cnt["dve"]), Tok(k.sem["act"], k.cnt["act"])]
    A.reset(m0)

    if stop <= 2:
        return finish()
    wiv = w_in.rearrange("(a p) n -> p a n", p=128)

    def hmov(kk, c0, n):
        return HT[:, kk, c0:c0 + n]

    mconv = A.mark()
    A32 = A.alloc([TT], F32)
    SIG = A.alloc([TT], F32)
    NP = NT * 128
    ACCA = A.alloc([NP], F32)
    ACCB = A.alloc([NP], F32)
    ACC1 = A.alloc([TQ], F32)
    ACC2 = A.alloc([TQ], F32)
    UE = A.alloc([NSB, 34], F32)
    YSA = A.alloc([NSB, 4], F32)
    YSB = A.alloc([NSB, 4], F32)
    STG = [A.alloc([4, 128], F32, parts=120) for _ in range(2)]
    UO = [A.alloc([128], F32) for _ in range(2)]
    stsl = [DmaSlot(k, "st0"), DmaSlot(k, "st1")]
    uosl = [DmaSlot(k, "uo0"), DmaSlot(k, "uo1")]
    cst = {"a32": None, "u_done": None, "stfree": [None, None], "uofree": [None, None], "ue_free": None,
           "acc_free": None, "sig_free": None}
    stv = st.rearrange("(g b) r d -> (b r) g d", b=4)
    U0 = 128 - (CONV_W - 1)

    def h_cua(cc):
        def h(banks, ch, tok):
            t = None
            for (p, c0, n) in psv(banks, ch):
                t = k.op("act", lambda e, p=p, c0=c0, n=n: e.activation(A32[:, c0:c0 + n], p, AF.Copy),
                         deps=[tok, cst["u_done"], cst.get("conv_done"), cst.get("un_done")])
            cst["a32"] = t
            return t
        return h

    def run_part_b():
        if cst.get("partB") is not None:
            f = cst["partB"]
            cst["partB"] = None
            f()

    def h_cub(cc):
        def h(banks, ch, tok):
            s = cc % 2
            flush_pending()
            run_part_b()
            sem, val = stsl[s].next()
            t_st = k.dma("sp", lambda e: e.dma_start(out=STG[s], in_=stv[:, :, cc * 128:(cc + 1) * 128]), sem, val,
                         deps=[cst["stfree"][s]] + HT_DONE)
            t_sig = None
            for (p, c0, n) in psv(banks, ch):
                t_sig = k.op("act", lambda e, p=p, c0=c0, n=n: e.activation(SIG[:, c0:c0 + n], p, AF.Sigmoid),
                             deps=[tok, cst["sig_free"]])
            t_u = k.op("dve", lambda e: e.tensor_tensor(A32, A32, SIG, ALU.mult), deps=[t_sig, cst["a32"]])
            t_u = k.op("dve", lambda e: e.tensor_scalar(A32[:, U0:128], A32[:, U0:128], FLG[:, 0:1], None, ALU.mult),
                       deps=[t_u] + CONST_TOKS)
            wv = lambda j: CW[:, cc, j:j + 1]
            ta = k.op("dve", lambda e: e.tensor_scalar(ACCA, A32[:, U0:U0 + NP], wv(0), CB[:, cc:cc + 1], ALU.mult, ALU.add),
                      deps=[t_u, cst["acc_free"]])
            tb = k.op("dve", lambda e: e.tensor_scalar(ACCB, A32[:, U0 + 1:U0 + 1 + NP], wv(1), None, ALU.mult),
                      deps=[t_u, cst["acc_free"]])
            for j in range(2, CONV_W):
                if j % 2 == 0:
                    ta = k.op("dve", lambda e, j=j: e.scalar_tensor_tensor(ACCA, A32[:, U0 + j:U0 + j + NP], wv(j), ACCA,
                                                                          ALU.mult, ALU.add), deps=[ta])
                else:
                    tb = k.op("dve", lambda e, j=j: e.scalar_tensor_tensor(ACCB, A32[:, U0 + j:U0 + j + NP], wv(j), ACCB,
                                                                          ALU.mult, ALU.add), deps=[tb])
            t_y = k.op("dve", lambda e: e.tensor_tensor(ACCA, ACCA, ACCB, ALU.add), deps=[ta, tb])
            cst["conv_done"] = t_y

            def later():
                bank = next_misc()
                lp = None
                for g in range(4):
                    lp = k.op("pe", lambda e, g=g: e.transpose(ps[:, bank, g * 120:(g + 1) * 120], STG[s][:, g, :],
                                                               IDF[0:120, 0:120]),
                              deps=[t_st, bank_free[bank]] + CONST_TOKS if g == 0 else (), sig=(g == 3))
                cst["stfree"][s] = lp
                t_ue = k.op("act", lambda e: e.activation(UE[:, :, 0:30], ps[:, bank, 0:480].rearrange("p (b r) -> p b r", r=30),
                                                          AF.Copy), deps=[lp, cst["ue_free"]])
                bank_free[bank] = t_ue
                t_un = k.op("act", lambda e: e.activation(UE[:, :, 30:34], A32[:, TT - NS:TT].rearrange("p (b t) -> p b t", t=4),
                                                          AF.Copy), deps=[t_u, cst["ue_free"]])
                cst["un_done"] = t_un
                sa = k.op("dve", lambda e: e.tensor_scalar(YSA, UE[:, :, 0:4], wv(0), CB[:, cc:cc + 1], ALU.mult, ALU.add),
                          deps=[t_ue, t_un, cst.get("ys_free")])
                sb = k.op("dve", lambda e: e.tensor_scalar(YSB, UE[:, :, 1:5], wv(1), None, ALU.mult),
                          deps=[t_ue, t_un, cst.get("ys_free")])
                for j in range(2, CONV_W):
                    if j % 2 == 0:
                        sa = k.op("dve", lambda e, j=j: e.scalar_tensor_tensor(YSA, UE[:, :, j:j + 4], wv(j), YSA,
                                                                              ALU.mult, ALU.add), deps=[sa])
                    else:
                        sb = k.op("dve", lambda e, j=j: e.scalar_tensor_tensor(YSB, UE[:, :, j:j + 4], wv(j), YSB,
                                                                              ALU.mult, ALU.add), deps=[sb])
                t_ys = k.op("dve", lambda e: e.tensor_tensor(YSA, YSA, YSB, ALU.add), deps=[sa, sb])
                cst["ue_free"] = t_ys
                bank2 = next_misc()
                t_tr = k.op("pe", lambda e: e.transpose(ps[:, bank2, 0:128], A32[:, TT - 128:TT], IDF),
                            deps=[t_u, bank_free[bank2]])
                t_uo = k.op("act", lambda e: e.activation(UO[s], ps[:, bank2, 0:128], AF.Copy),
                            deps=[t_tr, cst["uofree"][s]])
                bank_free[bank2] = t_uo
                cst["u_done"] = Tok(k.sem["pe"], t_tr.val)
                sem, val = uosl[s].next()
                d1 = k.dma("sp", lambda e: e.dma_start(out=conv_p[:, cc * 128:(cc + 1) * 128], in_=UO[s][34:64, :]),
                           sem, val, deps=[t_uo])
                sem, val = uosl[s].next()
                d2 = k.dma("sp", lambda e: e.dma_start(out=conv_sn[:, cc * 128:(cc + 1) * 128], in_=UO[s][64:128, :]),
                           sem, val, deps=[t_uo, d1])
                cst["uofree"][s] = d2
                out_toks.append(d2)
                cst["ue_free"] = t_ys
                cst["partB"] = lambda: part_b(t_ys)
            pending.append(later)

            def part_b(t_ys):
                ysv = YSA.rearrange("p b t -> p (b t)")
                gdep = [gate_state.get("last_pe")] if (GW_OK and cc >= HQ - ngm) else []
                assert not (GW_OK and cc >= HQ - ngm and gate_blocks), "gate blocks must be done before the overlay is reused"
                t1 = k.op("act", lambda e: e.activation(MIX[:, cc, 0:NP], ACCA, AF.Copy), deps=[t_y] + gdep)
                t2 = k.op("act", lambda e: e.activation(MIX[:, cc, NP:TQ], ysv, AF.Copy), deps=[t_ys] + gdep)
                if cc == 0:
                    t3 = k.op("dve", lambda e: e.tensor_copy(ACC1[:, 0:NP], ACCA), deps=[t_y])
                    t4 = k.op("dve", lambda e: e.tensor_copy(ACC1[:, NP:TQ], ysv), deps=[t_ys])
                else:
                    t3 = k.op("dve", lambda e: e.tensor_tensor(ACC1[:, 0:NP], ACC1[:, 0:NP], ACCA, ALU.add), deps=[t_y])
                    t4 = k.op("dve", lambda e: e.tensor_tensor(ACC1[:, NP:TQ], ACC1[:, NP:TQ], ysv, ALU.add), deps=[t_ys])
                t5 = k.op("act", lambda e: e.activation(SIG[:, 0:NP], ACCA, AF.Square), deps=[t_y, t_u])
                t6 = k.op("act", lambda e: e.activation(SIG[:, NP:TQ], ysv, AF.Square), deps=[t_ys, t_u])
                if cc == 0:
                    t7 = k.op("dve", lambda e: e.tensor_copy(ACC2, SIG[:, 0:TQ]), deps=[t5, t6])
                else:
                    t7 = k.op("dve", lambda e: e.tensor_tensor(ACC2, ACC2, SIG[:, 0:TQ], ALU.add), deps=[t5, t6])
                cst["sig_free"] = t7
                cst["acc_free"] = join([Tok(k.sem["dve"], k.cnt["dve"]), t5, t6])
                cst["ys_free"] = cst["acc_free"]
            return t_sig
        return h

    def h_cg(cc):
        def h(banks, ch, tok):
            return act_chunks(banks, ch, tok, lambda c0, n: MIX[:, HQ + cc, c0 - 128:c0 - 128 + n], AF.Silu)
        return h

    blocks = []
    ncc_ok = max(1, min(NCC, HQ - ngm)) if GW_OK else NCC
    gpc = -(-NKC // ncc_ok)
    for gb in gate_blocks:
        gb["own_deps"] = HT_DONE
    for cc in range(NCC):
        ng_left = gpc
        for (off, hf, t0) in ((c.oca, h_cua, 0), (c.ocb, h_cub, 0), (c.ocg, h_cg, 128)):
            col = off + cc * 128
            blocks.append(dict(w=(wiv[:, :, col:col + 128], NKC), mov=hmov, t0=t0, t1=TT, deps=HT_DONE,
                               handler=hf(cc)))
            if gate_blocks and ng_left > 0:
                blocks.append(gate_blocks.pop(0))
                ng_left -= 1
        while gate_blocks and ng_left > 0:
            blocks.append(gate_blocks.pop(0))
            ng_left -= 1
    blocks.extend(gate_blocks)
    del gate_blocks[:]
    run_blocks(blocks)
    flush_pending()
    run_part_b()
    flush_pending()
    CONV_DONE = [Tok(k.sem["dve"], k.cnt["dve"]), Tok(k.sem["act"], k.cnt["act"])]

    if stop <= 3:
        return finish()
    MEAN = A32[:, 0:TQ]
    RSTD = SIG[:, 0:TQ]
    t_ln = None
    for (src, which) in ((ACC1, 0), (ACC2, 1)):
        for (c0, n) in chunks(0, TQ):
            bank = next_misc()
            tp = k.op("pe", lambda e, bank=bank, c0=c0, n=n, src=src: e.matmul(ps[:, bank, 0:n], ONESF, src[:, c0:c0 + n],
                                                                              start=True, stop=True),
                      deps=CONV_DONE + [bank_free[bank]] + CONST_TOKS)
            if which == 0:
                t_ln = k.op("dve", lambda e, bank=bank, c0=c0, n=n: e.tensor_scalar(MEAN[:, c0:c0 + n], ps[:, bank, 0:n],
                                                                                  1.0 / DC, None, ALU.mult), deps=[tp])
            else:
                t_ln = k.op("dve", lambda e, bank=bank, c0=c0, n=n: e.tensor_scalar(RSTD[:, c0:c0 + n], ps[:, bank, 0:n],
                                                                                  1.0 / DC, None, ALU.mult), deps=[tp])
            bank_free[bank] = t_ln
    TMPL = ACCA[:, 0:NP] if NP >= TQ else ACC1
    TMPL = ACC1
    t_ln = k.op("dve", lambda e: e.tensor_tensor(TMPL, MEAN, MEAN, ALU.mult), deps=[t_ln])
    t_ln = k.op("dve", lambda e: e.tensor_tensor(RSTD, RSTD, TMPL, ALU.subtract), deps=[t_ln])
    t_ln = k.op("dve", lambda e: e.tensor_scalar(RSTD, RSTD, 0.0, LN_EPS, ALU.max, ALU.add), deps=[t_ln])
    t_ln = k.op("act", lambda e: e.activation(RSTD, RSTD, AF.Ln), deps=[t_ln])
    t_ln = k.op("act", lambda e: e.activation(RSTD, RSTD, AF.Exp, scale=-0.5), deps=[t_ln])
    TL = [ACC1, ACC2]
    tl_free = [None, None]
    act_toks = []
    for cc in range(NCC):
        i = cc % 2
        a1 = k.op("dve", lambda e, cc=cc, i=i: e.tensor_tensor(TL[i], MIX[:, cc, :], MEAN, ALU.subtract),
                  deps=[t_ln, tl_free[i]])
        a2 = k.op("dve", lambda e, i=i: e.tensor_tensor(TL[i], TL[i], RSTD, ALU.mult), deps=[a1])
        a3 = k.op("act", lambda e, cc=cc, i=i: e.activation(MIX[:, cc, :], TL[i], AF.Silu, bias=LB[:, cc:cc + 1],
                                                            scale=LG[:, cc:cc + 1]), deps=[a2])
        tl_free[i] = a3
        act_toks.append(a3)
    ACT_DONE = [Tok(k.sem["act"], k.cnt["act"])]

    wpv = w_pw2.rearrange("(a p) n -> p a n", p=128)

    def h_pw2(cc):
        def h(banks, ch, tok):
            t = None
            for (p, c0, n) in psv(banks, ch):
                t = k.op("dve", lambda e, p=p, c0=c0, n=n: e.tensor_tensor(MIX[:, HQ + cc, c0:c0 + n], p,
                                                                         MIX[:, HQ + cc, c0:c0 + n], ALU.mult), deps=[tok])
            return t
        return h

    blocks = []
    for cc in range(NCC):
        blocks.append(dict(w=(wpv[:, :, cc * 128:(cc + 1) * 128], NCC), mov=lambda kk, c0, n: MIX[:, kk, c0:c0 + n],
                           t0=0, t1=TQ, deps=ACT_DONE, handler=h_pw2(cc)))
    run_blocks(blocks)
    PW2_DONE = [Tok(k.sem["pe"], k.cnt["pe"])]
    CONVPH_DONE = PW2_DONE + [t for t in cst["uofree"] if t is not None] + gscr_toks[-2:]
    A.reset(base_mark)

    if stop <= 4:
        return finish()
    RAW = A.alloc([TT], F32)
    RAWV = A.alloc([TT], BF16)
    RVO = A.alloc([192], F32)
    SQ = A.alloc([TT], BF16)
    KT2 = [A.alloc([TT], BF16) for _ in range(2)]
    VT2 = [A.alloc([NTILE, 128], BF16) for _ in range(2)]
    Q4 = A.alloc([4, TQ], BF16)
    KN32 = A.alloc([192], F32)
    KO = [A.alloc([128], F32) for _ in range(2)]
    EX = A.alloc([2, 512], BF16)
    DEN = A.alloc([512], F32)
    BP = A.alloc([2, 4, 128], F32)
    kosl = [DmaSlot(k, "ko0"), DmaSlot(k, "ko1")]
    bpsl = DmaSlot(k, "bp")
    ast = {"raw_free": None, "rawv_free": None, "sq_free": None, "kt_free": [None, None],
           "vt_free": [None, None], "q_free": None, "kofree": [None, None], "koi": 0, "kt": None, "vt": None,
           "q": [None] * 4, "ga": {}, "bp_free": None, "ex_free": None, "den_free": None,
           "kn_free": None, "rvo_free": None}
    att_units = []

    def ko_out(src_ps, rows, dst, deps):
        i = ast["koi"] % 2
        ast["koi"] += 1
        c1 = k.op("act", lambda e: e.activation(KO[i][0:rows, :], src_ps, AF.Copy), deps=list(deps) + [ast["kofree"][i]])
        sem, val = kosl[i].next()
        d1 = k.dma("sp", lambda e: e.dma_start(out=dst, in_=KO[i][0:rows, :]), sem, val, deps=[c1])
        ast["kofree"][i] = d1
        out_toks.append(d1)
        return c1

    def qk_block(banks, ch, tok, t0, final):
        t_raw = None
        t_sq = None
        for (p, c0, n) in psv(banks, ch):
            t_raw = k.op("dve", lambda e, p=p, c0=c0, n=n: e.tensor_copy(RAW[:, c0:c0 + n], p), deps=[tok, ast["raw_free"]])
            t_sq = k.op("act", lambda e, c0=c0, n=n: e.activation(SQ[:, c0:c0 + n], RAW[:, c0:c0 + n], AF.Square),
                        deps=[t_raw, ast["sq_free"]])

        def later():
            sb = banks
            lastp = None
            for ci, (c0, n) in enumerate(ch):
                lastp = k.op("pe", lambda e, ci=ci, c0=c0, n=n: e.matmul(ps[:, sb[ci], 0:n], ONESB, SQ[:, c0:c0 + n],
                                                                        start=True, stop=True),
                             deps=[t_sq] + free_deps(sb[:len(ch)]) + CONST_TOKS if ci == 0 else ())
            ast["sq_free"] = lastp
            t_r = None
            for ci, (c0, n) in enumerate(ch):
                t_r = k.op("act", lambda e, ci=ci, n=n: e.activation(ps[:, sb[ci], 0:n], ps[:, sb[ci], 0:n], AF.Ln,
                                                                     bias=SMALL[:, 9:10], scale=1.0 / 128), deps=[lastp])
            t_r2 = None
            for ci, (c0, n) in enumerate(ch):
                t_r2 = k.op("act", lambda e, ci=ci, n=n: e.activation(ps[:, sb[ci], 0:n], ps[:, sb[ci], 0:n], AF.Exp,
                                                                      scale=-0.5), deps=[t_r])
            rsv = [(ps[:, sb[ci], 0:n], c0, n) for ci, (c0, n) in enumerate(ch)]
            ftok = final(t_r2, t_raw, rsv)
            set_free(sb[:len(ch)], ftok)
        pending.append(later)
        return t_raw

    def h_k(n):
        def h(banks, ch, tok):
            sl = n % 2
            KT = KT2[sl]

            def final(t_r, t_raw, rsv):
                if ast["kt_free"][sl] is None and n >= 2:
                    drain_units()
                t1 = None
                for (rp, c0, nn) in rsv:
                    t1 = k.op("dve", lambda e, rp=rp, c0=c0, nn=nn: e.scalar_tensor_tensor(
                        KT[:, c0:c0 + nn], RAW[:, c0:c0 + nn], KG[:, 0:1], rp, ALU.mult, ALU.mult),
                        deps=[t_r, t_raw, ast["kt_free"][sl]] + CONST_TOKS)
                t2 = None
                for (rp, c0, nn) in rsv:
                    lo, hi = max(c0, TT - 192), c0 + nn
                    if hi > lo:
                        t2 = k.op("dve", lambda e, rp=rp, c0=c0, lo=lo, hi=hi: e.scalar_tensor_tensor(
                            KN32[:, lo - (TT - 192):hi - (TT - 192)], RAW[:, lo:hi], KG[:, 0:1], rp[:, lo - c0:hi - c0],
                            ALU.mult, ALU.mult), deps=[t_r, t_raw, ast["kn_free"]])
                t3 = k.op("dve", lambda e: e.tensor_copy(KSN[:, n, :], KT[:, TT - NS:TT]), deps=[t1])
                ast["raw_free"] = t2
                ast["kt"] = t3
                ast["kt_free"][sl] = None

                def later2():
                    bank = next_misc()
                    k.op("pe", lambda e: e.transpose(ps[:, bank, 0:128], KN32[:, 0:128], IDF),
                         deps=[t2, bank_free[bank]] + CONST_TOKS, sig=False)
                    p2 = k.op("pe", lambda e: e.transpose(ps[0:NS, bank, 128:256], KN32[:, 128:192], IDF))
                    ast["kn_free"] = p2
                    ko_out(ps[:, bank, 0:128], 128, k_p[:, n * 128:(n + 1) * 128], [p2])
                    c2 = ko_out(ps[0:NS, bank, 128:256], NS, k_sn[:, n * 128:(n + 1) * 128], [p2])
                    bank_free[bank] = c2
                pending.append(later2)
                return t2
            return qk_block(banks, ch, tok, 0, final)
        return h

    def h_v(n):
        def h(banks, ch, tok):
            sl = n % 2
            VT = VT2[sl]
            t_raw = None
            for (p, c0, n_) in psv(banks, ch):
                t_raw = k.op("dve", lambda e, p=p, c0=c0, n_=n_: e.tensor_copy(RAWV[:, c0:c0 + n_], p),
                             deps=[tok, ast["rawv_free"]])
            t_rvo = None
            for (p, c0, n_) in psv(banks, ch):
                lo, hi = max(c0, TT - 192), c0 + n_
                if hi > lo:
                    t_rvo = k.op("dve", lambda e, p=p, c0=c0, lo=lo, hi=hi: e.tensor_copy(
                        RVO[:, lo - (TT - 192):hi - (TT - 192)], p[:, lo - c0:hi - c0]),
                        deps=[tok, ast["rvo_free"]])
            rel = t_rvo

            def later():
                if ast["vt_free"][sl] is None and n >= 2:
                    drain_units()
                lastpe = None
                tv = None
                for g0 in range(0, NTILE, 8):
                    bank = next_misc()
                    g1 = min(NTILE, g0 + 8)
                    for ti in range(g0, g1):
                        rows = 128 if ti < NTILE - 1 else NS
                        lastpe = k.op("pe", lambda e, ti=ti, rows=rows, bank=bank, g0=g0: e.transpose(
                            ps_bf[0:rows, bank, (ti - g0) * 128:(ti - g0 + 1) * 128], RAWV[:, ti * 128:ti * 128 + rows], IDB),
                            deps=[t_raw, bank_free[bank], ast["vt_free"][sl]] + CONST_TOKS if ti == g0 else (),
                            sig=(ti == g1 - 1))
                    for ti in range(g0, g1):
                        rows = 128 if ti < NTILE - 1 else NS
                        src = ps_bf[0:rows, bank, (ti - g0) * 128:(ti - g0 + 1) * 128]
                        tv = k.op("act", lambda e, ti=ti, rows=rows, src=src: e.activation(VT[0:rows, ti, :], src, AF.Copy),
                                  deps=[lastpe])
                    bank_free[bank] = tv
                ast["rawv_free"] = lastpe
                c3 = k.op("dve", lambda e: e.tensor_copy(VSN[:, n, :], VT[0:NS, NTILE - 1, :]), deps=[tv])
                ast["vt"] = c3
                ast["vt_free"][sl] = None
                bank = next_misc()
                k.op("pe", lambda e: e.transpose(ps[:, bank, 0:128], RVO[:, 0:128], IDF),
                     deps=[t_rvo, bank_free[bank]] + CONST_TOKS, sig=False)
                p2 = k.op("pe", lambda e: e.transpose(ps[0:NS, bank, 128:256], RVO[:, 128:192], IDF))
                ast["rvo_free"] = p2
                ko_out(ps[:, bank, 0:128], 128, v_p[:, n * 128:(n + 1) * 128], [p2])
                c2 = ko_out(ps[0:NS, bank, 128:256], NS, v_sn[:, n * 128:(n + 1) * 128], [p2])
                bank_free[bank] = c2
            pending.append(later)
            return rel
        return h

    def h_q(h_idx):
        j = h_idx % 4

        def h(banks, ch, tok):
            def final(t_r, t_raw, rsv):
                if j == 0:
                    drain_units()
                t1 = None
                for (rp, c0, nn) in rsv:
                    t1 = k.op("dve", lambda e, rp=rp, c0=c0, nn=nn: e.scalar_tensor_tensor(
                        Q4[:, j, c0 - 128:c0 - 128 + nn], RAW[:, c0:c0 + nn], QG[:, 0:1], rp, ALU.mult, ALU.mult),
                        deps=[t_r, t_raw, ast["q_free"]] + CONST_TOKS)
                t2 = k.op("dve", lambda e: e.tensor_copy(QS[:, h_idx, :], Q4[:, j, TQ - NS:TQ]), deps=[t1])
                ast["raw_free"] = t1
                ast["q"][j] = t2
                if j == 3:
                    schedule_attention(h_idx // 4)
                return t1
            return qk_block(banks, ch, tok, 128, final)
        return h

    def h_ga(h_idx):
        def h(banks, ch, tok):
            t = act_chunks(banks, ch, tok, lambda c0, n: MIX[:, h_idx, c0 - 128:c0 - 128 + n], AF.Silu)
            ast["ga"][h_idx] = t
            return t
        return h

    def run_unit(n, i, sl, deps0, tb):
        KT, VT = KT2[sl], VT2[sl]
        lp = None
        for blk in range(2):
            bank = 6 + blk
            lp = k.op("pe", lambda e, blk=blk, bank=bank: e.matmul(
                ps[:, bank, :].rearrange("p (h q) -> p h q", h=4), KT[:, (i + blk) * 128:(i + blk + 1) * 128],
                Q4[:, :, i * 128:(i + 1) * 128], start=True, stop=True),
                deps=deps0 + [bank_free[6], bank_free[7]] if blk == 0 else ())
        t_t = None
        for blk in range(2):
            bias = BP[:, blk].rearrange("p h q -> p (h q)")
            t_t = k.op("dve", lambda e, blk=blk, bias=bias: e.scalar_tensor_tensor(
                ps[:, 6 + blk, :], ps[:, 6 + blk, :], ISD, bias, ALU.mult, ALU.add), deps=[lp, tb])
        t_e = k.op("act", lambda e: e.activation(EX, ps[:, 6:8, :], AF.Exp), deps=[t_t, ast["ex_free"]])
        if i == 0:
            t_e = k.op("dve", lambda e: e.tensor_scalar(EX[:, 0, :], EX[:, 0, :], FLG[:, 0:1], None, ALU.mult),
                       deps=[t_e] + CONST_TOKS)
        bank_free[6] = t_e
        bank_free[7] = t_e

        def stage2():
            k.op("pe", lambda e: e.matmul(ps[:, 6, :], ONESB, EX[:, 0, :], start=True, stop=False),
                 deps=[t_e, bank_free[6], bank_free[7]] + CONST_TOKS, sig=False)
            p1 = k.op("pe", lambda e: e.matmul(ps[:, 6, :], ONESB, EX[:, 1, :], start=False, stop=True))
            k.op("pe", lambda e: e.matmul(ps[:, 7, :], VT[:, i, :], EX[:, 0, :], start=True, stop=False), sig=False)
            p2 = k.op("pe", lambda e: e.matmul(ps[:, 7, :], VT[:, i + 1, :], EX[:, 1, :], start=False, stop=True))
            ast["ex_free"] = p2
            d1 = k.op("dve", lambda e: e.tensor_tensor(
                DEN.rearrange("p (h q) -> p h q", h=4), ps[:, 6, :].rearrange("p (h q) -> p h q", h=4),
                ESK[:, 4 * n:4 * n + 4].unsqueeze(2).to_broadcast([128, 4, 128]), ALU.add),
                deps=[p1, ast["den_free"], t_c4])
            d2 = k.op("act", lambda e: e.activation(DEN, DEN, AF.Ln), deps=[d1])
            d2 = k.op("act", lambda e: e.activation(DEN, DEN, AF.Exp, scale=-1.0), deps=[d2])
            d3 = k.op("dve", lambda e: e.tensor_tensor(DEN, ps[:, 7, :], DEN, ALU.mult), deps=[d2, p2])
            bank_free[6] = d3
            bank_free[7] = d3
            mv = MIX[:, 4 * n:4 * n + 4, i * 128:(i + 1) * 128]
            d4 = k.op("dve", lambda e: e.tensor_tensor(mv, DEN.rearrange("p (h q) -> p h q", h=4), mv, ALU.mult),
                      deps=[d3] + [ast["ga"][4 * n + jj] for jj in range(4)])
            ast["den_free"] = d4
            if i == NT - 1:
                ast["bp_free"] = d4
                pt = Tok(k.sem["pe"], p2.val)
                ast["kt_free"][sl] = pt
                ast["vt_free"][sl] = pt
                ast["q_free"] = pt
        return stage2

    def schedule_attention(n):
        sl = n % 2
        deps0 = [ast["kt"], ast["vt"]] + list(ast["q"])
        sem, val = bpsl.next()
        tb = k.dma("sp", lambda e: e.dma_start(out=BP, in_=biasp[n]), sem, val, deps=[ast["bp_free"]] + CONVPH_DONE)
        for i in range(NT):
            att_units.append((n, i, sl, deps0, tb))

    inflight = {"s2": None}

    def att_tick():
        if inflight["s2"] is not None:
            s2 = inflight["s2"]
            inflight["s2"] = None
            s2()
        elif att_units:
            u = att_units.pop(0)
            inflight["s2"] = run_unit(*u)

    tickers.append(att_tick)

    def drain_units():
        while att_units or inflight["s2"] is not None:
            att_tick()

    def with_unit(hf):
        return hf

    blocks = []
    for n in range(HKV):
        blocks.append(dict(w=(wiv[:, :, c.ok + n * 128:c.ok + (n + 1) * 128], NKC), mov=hmov, t0=0, t1=TT,
                           deps=HT_DONE + CONVPH_DONE, handler=with_unit(h_k(n))))
        blocks.append(dict(w=(wiv[:, :, c.ov + n * 128:c.ov + (n + 1) * 128], NKC), mov=hmov, t0=0, t1=TT,
                           deps=HT_DONE + CONVPH_DONE, handler=with_unit(h_v(n))))
        for j in range(4):
            hh = 4 * n + j
            blocks.append(dict(w=(wiv[:, :, c.oga + hh * 128:c.oga + (hh + 1) * 128], NKC), mov=hmov, t0=128, t1=TT,
                               deps=HT_DONE + CONVPH_DONE, handler=with_unit(h_ga(hh))))
        for j in range(4):
            hh = 4 * n + j
            blocks.append(dict(w=(wiv[:, :, c.oq + hh * 128:c.oq + (hh + 1) * 128], NKC), mov=hmov, t0=128, t1=TT,
                               deps=HT_DONE + CONVPH_DONE, handler=with_unit(h_q(hh))))
    run_blocks(blocks)
    while att_units or pending or inflight["s2"] is not None:
        flush_pending()
    del tickers[:]
    ATT_P_DONE = [Tok(k.sem["dve"], k.cnt["dve"]), Tok(k.sem["pe"], k.cnt["pe"]), Tok(k.sem["act"], k.cnt["act"])]
    ATT_P_DONE += [t for t in ast["kofree"] if t is not None]

    if stop <= 5:
        return finish()
    ht_bytes = NKC * TT * 2
    if ht_bytes >= 40000:
        B = Arena(pool_t, 0, ht_bytes)
    else:
        A.reset(base_mark)
        B = A
    CK = [B.alloc([NSB, 128], BF16) for _ in range(2)]
    CV = [B.alloc([NSB, 128], BF16) for _ in range(2)]
    KCT = B.alloc([NSB, 128], BF16)
    TC = B.alloc([4 * NS], F32)
    TN = B.alloc([4 * NS], F32, parts=NS)
    EC = B.alloc([4 * NS], BF16)
    EN = B.alloc([4 * NS], BF16, parts=NS)
    BC = B.alloc([4 * NS], F32)
    BN = B.alloc([4 * NS], F32, parts=NS)
    DN = B.alloc([4 * NS], F32)
    cksl = [DmaSlot(k, "ck0"), DmaSlot(k, "ck1")]
    cvsl = [DmaSlot(k, "cv0"), DmaSlot(k, "cv1")]
    bcsl = DmaSlot(k, "bc")
    sfree = {"ck": [None, None], "cv": [None, None], "kct": None, "tc": None, "ec": None, "bc": None, "dn": None}
    ckv = ck.rearrange("b j n d -> j b n d")
    cvv = cv.rearrange("b j n d -> j b n d")
    for n in range(HKV):
        s = n % 2
        sem, val = cksl[s].next()
        t_ck = k.dma("pool", lambda e, s=s, n=n: e.dma_start(out=CK[s], in_=ckv[:, :, n, :]), sem, val,
                     deps=ATT_P_DONE + [sfree["ck"][s]])
        sem, val = cvsl[s].next()
        t_cv = k.dma("pool", lambda e, s=s, n=n: e.dma_start(out=CV[s], in_=cvv[:, :, n, :]), sem, val,
                     deps=ATT_P_DONE + [sfree["cv"][s]])
        sem, val = bcsl.next()
        t_bc = k.dma("sp", lambda e, n=n: e.dma_start(out=BC, in_=biasc[n]), sem, val, deps=ATT_P_DONE + [sfree["bc"]])
        sem, val = bcsl.next()
        t_bn = k.dma("sp", lambda e, n=n: e.dma_start(out=BN, in_=biasn[n]), sem, val, deps=[t_bc])
        t_kct = None
        for g0 in range(0, NSB, 8):
            bank = next_misc()
            lp = None
            for b in range(g0, g0 + 8):
                lp = k.op("pe", lambda e, b=b, bank=bank, g0=g0, s=s: e.transpose(
                    ps_bf[:, bank, (b - g0) * 128:(b - g0 + 1) * 128], CK[s][:, b, :], IDB),
                    deps=[t_ck, bank_free[bank], sfree["kct"]] if b == g0 else (), sig=(b == g0 + 7))
            t_kct = k.op("act", lambda e, bank=bank, g0=g0: e.activation(
                KCT[:, g0:g0 + 8, :], ps_bf[:, bank, :].rearrange("p (a b) -> p a b", b=128), AF.Copy), deps=[lp])
            bank_free[bank] = t_kct
        sfree["ck"][s] = Tok(k.sem["pe"], k.cnt["pe"])
        lp = None
        for b in range(NSB):
            lp = k.op("pe", lambda e, b=b, n=n: e.matmul(
                ps[:, 6, 16 * b:16 * b + 16], KCT[:, b, :],
                QS[:, 4 * n:4 * n + 4, 4 * b:4 * b + 4], start=(b == 0), stop=(b == NSB - 1)),
                deps=[t_kct, bank_free[6], bank_free[7]] + ATT_P_DONE if b == 0 else (), sig=False)
        lp = k.op("pe", lambda e, n=n: e.matmul(ps[0:NS, 7, 0:4 * NS], KSN[:, n, :],
                                                QS[:, 4 * n:4 * n + 4, :].rearrange("p h (b t) -> p b h t", t=4),
                                                start=True, stop=True))
        sfree["kct"] = lp
        t1 = k.op("dve", lambda e: e.scalar_tensor_tensor(TC, ps[:, 6, 0:4 * NS], ISD, BC, ALU.mult, ALU.add),
                  deps=[lp, t_bn, sfree["tc"]])
        t2 = k.op("dve", lambda e: e.scalar_tensor_tensor(TN, ps[0:NS, 7, 0:4 * NS], ISD, BN, ALU.mult, ALU.add),
                  deps=[lp, t_bn])
        sfree["bc"] = t2
        bank_free[6] = t2
        bank_free[7] = t2
        e1 = k.op("act", lambda e: e.activation(EC, TC, AF.Exp), deps=[t1, sfree["ec"]])
        e2 = k.op("act", lambda e: e.activation(EN, TN, AF.Exp), deps=[t2])
        sfree["tc"] = e2
        p1 = k.op("pe", lambda e: e.matmul(ps[:, 6, 0:4 * NS], ONESB, EC, start=True, stop=False),
                  deps=[e1, e2, bank_free[6], bank_free[7]], sig=False)
        p1 = k.op("pe", lambda e: e.matmul(ps[:, 6, 0:4 * NS], ONESB[0:NS, :], EN, start=False, stop=True))
        k.op("pe", lambda e, n=n: e.matmul(ps[:, 7, 0:4 * NS], VSN[:, n, :], EN, start=True, stop=False),
             deps=[t_cv], sig=False)
        p2 = None
        for b in range(NSB):
            p2 = k.op("pe", lambda e, b=b, s=s: e.matmul(
                ps[:, 7, 16 * b:16 * b + 16], CV[s][:, b, :], EC[:, 16 * b:16 * b + 16], start=False, stop=(b == NSB - 1)),
                sig=(b == NSB - 1))
        sfree["ec"] = p2
        sfree["cv"][s] = p2
        d1 = k.op("dve", lambda e, n=n: e.tensor_tensor(
            DN.rearrange("p (b h t) -> p b h t", b=NSB, h=4), ps[:, 6, 0:4 * NS].rearrange("p (b h t) -> p b h t", b=NSB, h=4),
            ESK[:, 4 * n:4 * n + 4].unsqueeze(1).unsqueeze(3).to_broadcast([128, NSB, 4, 4]), ALU.add),
            deps=[p1, sfree["dn"]])
        d2 = k.op("act", lambda e: e.activation(DN, DN, AF.Ln), deps=[d1])
        d2 = k.op("act", lambda e: e.activation(DN, DN, AF.Exp, scale=-1.0), deps=[d2])
        d3 = k.op("dve", lambda e: e.tensor_tensor(DN, ps[:, 7, 0:4 * NS], DN, ALU.mult), deps=[d2, p2])
        bank_free[6] = d3
        bank_free[7] = d3
        mv = MIX[:, 4 * n:4 * n + 4, TQ - NS:TQ].rearrange("p h (b t) -> p h b t", t=4)
        d4 = k.op("dve", lambda e, mv=mv: e.tensor_tensor(mv, DN.rearrange("p (b h t) -> p h b t", b=NSB, h=4), mv, ALU.mult),
                  deps=[d3])
        sfree["dn"] = d4
    MIX_DONE = [Tok(k.sem["dve"], k.cnt["dve"]), Tok(k.sem["pe"], k.cnt["pe"]), Tok(k.sem["act"], k.cnt["act"])]

    if stop <= 6:
        return finish()
    OW = c.OW
    NOS = D // OW
    ov = Arena(pool_t, 0, ht_bytes)
    fr = Arena(pool_t, base_mark, POOLB)

    def alloc_any(shape, dt, parts=128):
        esz = {F32: 4, BF16: 2}[dt]
        nb = int(np.prod(shape)) * esz
        for ar in (ov, fr):
            st_ = (ar.cur + 63) // 64 * 64
            if st_ + nb <= ar.hi:
                return ar.alloc(shape, dt, parts)
        raise AssertionError("no room for w_out phase buffers")

    NXR = 3
    early = ht_bytes >= 40000 and B.peak <= NMC * OW * 2 and 2 * NMC * OW * 2 + 2 * NXR * OW * 4 <= ht_bytes
    if early:
        wo1 = ov.alloc([NMC, OW], BF16)
        wo0 = ov.alloc([NMC, OW], BF16)
        WO = [wo0, wo1]
        XR = [ov.alloc([OW], F32) for _ in range(NXR)]
        YO = [ov.alloc([OW], F32) for _ in range(NXR)]
        GP = fr.alloc([D], F32)
        GS_ = fr.alloc([D], F32, parts=NS)
        PRE = ATT_P_DONE
    else:
        WO = [alloc_any([NMC, OW], BF16) for _ in range(2)]
        GP = alloc_any([D], F32)
        GS_ = alloc_any([D], F32, parts=NS)
        XR = [alloc_any([OW], F32) for _ in range(NXR)]
        YO = [alloc_any([OW], F32) for _ in range(NXR)]
        PRE = MIX_DONE
    wosl = [DmaSlot(k, "wo0"), DmaSlot(k, "wo1")]
    xrsl = [DmaSlot(k, f"xr{i}") for i in range(NXR)]
    yosl = [DmaSlot(k, f"yo{i}") for i in range(NXR)]
    gpsl = DmaSlot(k, "gp")
    sem, val = gpsl.next()
    t_gp = k.dma("sp", lambda e: e.dma_start(out=GP, in_=gscr[64, :].partition_broadcast(128)),
                 sem, val, deps=gscr_toks[-2:] + PRE)
    sem, val = gpsl.next()
    t_gs = k.dma("sp", lambda e: e.dma_start(out=GS_, in_=gscr[0:NS, :]), sem, val, deps=[t_gp])
    wov = w_out.rearrange("(a p) n -> p a n", p=128)
    wofree = [None, None]
    xrfree = [None] * NXR
    yofree = [None] * NXR
    wo_toks = {}

    def issue_wo(sidx):
        s = sidx % 2
        sem, val = wosl[s].next()
        wo_toks[sidx] = k.dma("pool", lambda e: e.dma_start(out=WO[s], in_=wov[:, :, sidx * OW:(sidx + 1) * OW]), sem, val,
                              deps=(PRE if sidx == 0 else MIX_DONE) + [wofree[s]])
    issue_wo(0)
    it = 0
    bank_rr = 0
    for sidx in range(NOS):
        if sidx + 1 < NOS:
            issue_wo(sidx + 1)
        s = sidx % 2
        lastpe = None
        for ti in range(NT + 1):
            rows = 128 if ti < NT else NS
            m0_ = ti * 128
            r = it % NXR
            it += 1
            bank = bank_rr % 8
            bank_rr += 1
            sem, val = xrsl[r].next()
            t_xr = k.dma("sp", lambda e, r=r, rows=rows, m0_=m0_, sidx=sidx: e.dma_start(
                out=XR[r][0:rows, :], in_=x[128 + m0_:128 + m0_ + rows, sidx * OW:(sidx + 1) * OW]), sem, val,
                deps=[xrfree[r], t_gs])
            for kc in range(NMC):
                lastpe = k.op("pe", lambda e, kc=kc, bank=bank, rows=rows, m0_=m0_, s=s: e.matmul(
                    ps[0:rows, bank, 0:OW], MIX[:, kc, m0_:m0_ + rows], WO[s][:, kc, :], start=(kc == 0), stop=(kc == NMC - 1)),
                    deps=[wo_toks[sidx], bank_free[bank]] + MIX_DONE if kc == 0 else (), sig=(kc == NMC - 1))
            gate = (GP[0:rows, sidx * OW:(sidx + 1) * OW] if ti < NT else GS_[0:rows, sidx * OW:(sidx + 1) * OW])
            o1 = k.op("dve", lambda e, r=r, rows=rows, bank=bank, gate=gate: e.tensor_tensor(
                YO[r][0:rows, :], ps[0:rows, bank, 0:OW], gate, ALU.mult), deps=[lastpe, t_gs, yofree[r]])
            bank_free[bank] = o1
            o2 = k.op("dve", lambda e, r=r, rows=rows: e.tensor_tensor(YO[r][0:rows, :], YO[r][0:rows, :], XR[r][0:rows, :],
                                                                     ALU.add), deps=[o1, t_xr])
            xrfree[r] = o2
            dst = (y_p[m0_:m0_ + rows, sidx * OW:(sidx + 1) * OW] if ti < NT else y_s[:, sidx * OW:(sidx + 1) * OW])
            sem, val = yosl[r].next()
            d = k.dma("sp", lambda e, r=r, rows=rows, dst=dst: e.dma_start(out=dst, in_=YO[r][0:rows, :]), sem, val, deps=[o2])
            yofree[r] = d
            out_toks.append(d)
        wofree[s] = lastpe

    return finish()


def alibi_slopes(HQ):
    return (2.0 ** (-8.0 * np.arange(1, HQ + 1) / HQ)).astype(np.float32)


def const_tables(cfg, first_chunk):
    HQ, HKV = cfg.HQ, cfg.HKV
    sl = alibi_slopes(HQ).reshape(HKV, 4)
    kj = np.arange(128)[:, None].astype(np.float32)
    qi = np.arange(128)[None, :].astype(np.float32)
    dist_prev = 128 + qi - kj
    dist_own = qi - kj
    mask_prev = np.where(dist_prev < WINDOW, 0.0, NEG).astype(np.float32)
    mask_own = np.where(dist_own >= 0, 0.0, NEG).astype(np.float32)
    biasp = np.zeros((HKV, 128, 2, 4, 128), np.float32)
    for g in range(HKV):
        for h in range(4):
            biasp[g, :, 0, h, :] = -sl[g, h] * dist_prev + mask_prev
            biasp[g, :, 1, h, :] = -sl[g, h] * dist_own + mask_own
    biasp0 = biasp[:, :, 0].copy()
    if first_chunk:
        biasp0[:] = NEG
    j = np.arange(128)[:, None].astype(np.float32)
    t = np.arange(4)[None, :].astype(np.float32)
    dc = t + 128 - j
    mc = np.where(dc < WINDOW, 0.0, NEG).astype(np.float32)
    biasc = np.zeros((HKV, 128, NSB, 4, 4), np.float32)
    for g in range(HKV):
        for h in range(4):
            biasc[g, :, :, h, :] = (-sl[g, h] * dc + mc)[:, None, :]
    bk = np.repeat(np.arange(NSB), 4)[:, None]
    tk = np.tile(np.arange(4), NSB)[:, None].astype(np.float32)
    biasn = np.zeros((HKV, NS, NSB, 4, 4), np.float32)
    for g in range(HKV):
        for h in range(4):
            for b in range(NSB):
                for tt in range(4):
                    d = tt - tk[:, 0]
                    ok = (bk[:, 0] == b) & (d >= 0)
                    biasn[g, :, b, h, tt] = np.where(ok, -sl[g, h] * d, NEG)
    return dict(biasp=biasp, biasp0=biasp0, biasc=biasc.reshape(HKV, 128, 4 * NS),
                biasn=biasn.reshape(HKV, NS, 4 * NS), identf=np.eye(128, dtype=np.float32))


def make_in_maps(cfg, inp):
    c = cfg
    cps = c.n_cores // c.batch
    L = c.NT * 128
    xp, xs = np.asarray(inp["x_prompt"]), np.asarray(inp["x_sample"])
    cp, csm = np.asarray(inp["c_prompt"]), np.asarray(inp["c_sample"])
    shared = dict(
        w_ada=np.ascontiguousarray(inp["w_ada"][0]), b_ada=np.ascontiguousarray(inp["b_ada"][0]),
        norm_g=np.ascontiguousarray(inp["norm_g"][0]), w_in=np.ascontiguousarray(inp["w_in"][0]),
        q_g=np.ascontiguousarray(inp["q_norm_g"][0]), k_g=np.ascontiguousarray(inp["k_norm_g"][0]),
        sinks=np.ascontiguousarray(inp["sinks"][0]), conv_w=np.ascontiguousarray(inp["conv_w"][0]),
        conv_b=np.ascontiguousarray(inp["conv_b"][0]), ln_g=np.ascontiguousarray(inp["ln_g"][0]),
        ln_b=np.ascontiguousarray(inp["ln_b"][0]), w_pw2=np.ascontiguousarray(inp["w_pw2"][0]),
        w_out=np.ascontiguousarray(inp["w_out"][0]))
    shared = {kk: np.asarray(v, dtype=np.float32) for kk, v in shared.items()}
    tabs = {True: const_tables(c, True), False: const_tables(c, False)}
    maps = []
    for core in range(c.n_cores):
        b, ch = core // cps, core % cps
        first = (ch == 0)
        x = np.zeros((c.TT, c.D), np.float32)
        if not first:
            x[0:128] = xp[b, ch * L - 128:ch * L]
        x[128:128 + L] = xp[b, ch * L:(ch + 1) * L]
        sb = slice(core * NSB, (core + 1) * NSB)
        x[128 + L:] = xs[sb].reshape(NS, c.D)
        csel = np.zeros((128, c.D), np.float32)
        csel[0:NS] = np.repeat(csm[sb], 4, axis=0)
        csel[64] = cp[b]
        m = dict(shared)
        m.update(tabs[first])
        m.update(x=x, csel=csel,
                 ck=np.ascontiguousarray(inp["cache_k_win"][0][sb], dtype=np.float32),
                 cv=np.ascontiguousarray(inp["cache_v_win"][0][sb], dtype=np.float32),
                 st=np.ascontiguousarray(inp["state_conv"][0][sb], dtype=np.float32),
                 flag=np.full((128, 1), 0.0 if first else 1.0, np.float32))
        maps.append(m)
    return maps


def assemble(cfg, res):
    c = cfg
    cps = c.n_cores // c.batch
    L = c.NT * 128
    r = [{kk: np.asarray(v) for kk, v in rr.items()} for rr in res]
    y_p = np.stack([np.concatenate([r[b * cps + ch]["y_p"] for ch in range(cps)], 0) for b in range(c.batch)], 0)
    y_s = np.concatenate([rr["y_s"].reshape(NSB, 4, c.D) for rr in r], 0)
    lastc = [b * cps + cps - 1 for b in range(c.batch)]
    k_p = np.stack([r[i]["k_p"].reshape(128, c.HKV, 128) for i in lastc], 0)[None]
    v_p = np.stack([r[i]["v_p"].reshape(128, c.HKV, 128) for i in lastc], 0)[None]
    conv_p = np.stack([r[i]["conv_p"] for i in lastc], 0)[None]
    k_s = np.concatenate([np.concatenate([rr["k_so"], rr["k_sn"].reshape(NSB, 4, c.KVW)], 1) for rr in r], 0)
    v_s = np.concatenate([np.concatenate([rr["v_so"], rr["v_sn"].reshape(NSB, 4, c.KVW)], 1) for rr in r], 0)
    conv_s = np.concatenate([np.concatenate([rr["conv_so"], rr["conv_sn"].reshape(NSB, 4, c.DC)], 1) for rr in r], 0)
    k_s = k_s.reshape(-1, WINDOW, c.HKV, 128)[None]
    v_s = v_s.reshape(-1, WINDOW, c.HKV, 128)[None]
    conv_s = conv_s[None]
    f = lambda a: np.ascontiguousarray(a, dtype=np.float32)
    return (f(y_p), f(y_s), f(k_p), f(v_p), f(conv_p), f(k_s), f(v_s), f(conv_s))


_NC_CACHE = {}


def kernel(**inputs):
    cfg = Cfg()
    if "nc" not in _NC_CACHE:
        _NC_CACHE["nc"] = build(cfg)
    nc = _NC_CACHE["nc"]
    maps = make_in_maps(cfg, inputs)
    res = run_bass_kernel_spmd(nc, maps, core_ids=list(range(cfg.n_cores)))
    return assemble(cfg, res.results)
```

```python
import numpy as np
import concourse.bass as bass
import concourse.mybir as mybir
from concourse.bass_utils import run_bass_kernel_spmd

F32 = mybir.dt.float32
BF16 = mybir.dt.bfloat16
U8 = mybir.dt.uint8
AF = mybir.ActivationFunctionType
ALU = mybir.AluOpType

WINDOW = 128
CONV_W = 31
NSB = 16
NS = 64
RMS_EPS = 1e-6
LN_EPS = 1e-5
NEG = -30000.0
ISD = 128 ** -0.5


class Cfg:
    def __init__(self, D=4096, HQ=16, DC=2048, NT=8, n_cores=8, batch=2, seq=4096):
        self.D, self.HQ, self.DC, self.NT = D, HQ, DC, NT
        self.n_cores, self.batch, self.seq = n_cores, batch, seq
        self.NKC = D // 128
        self.HKV = HQ // 4
        self.NCC = DC // 128
        self.DATT = HQ * 128
        self.KVW = self.HKV * 128
        self.DIN = self.DATT + 2 * self.KVW + self.DATT + 3 * DC
        self.DMIX = self.DATT + DC
        self.NMC = self.DMIX // 128
        self.TT = 128 + NT * 128 + NS
        self.TQ = self.TT - 128
        self.oq = 0
        self.ok = self.DATT
        self.ov = self.ok + self.KVW
        self.oga = self.ov + self.KVW
        self.oca = self.oga + self.DATT
        self.ocb = self.oca + DC
        self.ocg = self.ocb + DC
        self.OW = 512 if D >= 512 else D


class Tok:
    __slots__ = ("sem", "val")

    def __init__(self, sem, val):
        self.sem, self.val = sem, val


class Arena:
    def __init__(self, pool, lo, hi):
        self.pool, self.lo, self.hi, self.cur = pool, lo, hi, lo
        self.peak = lo

    def alloc(self, free_shape, dtype, parts=128):
        esz = {F32: 4, BF16: 2, U8: 1}[dtype]
        n = int(np.prod(free_shape))
        nbytes = n * esz
        start = (self.cur + 63) // 64 * 64
        assert start + nbytes <= self.hi, f"arena overflow need {start + nbytes} hi {self.hi}"
        self.cur = start + nbytes
        self.peak = max(self.peak, self.cur)
        v = self.pool[0:parts, start:start + nbytes].bitcast(dtype)
        if len(free_shape) == 2:
            v = v.rearrange("p (a b) -> p a b", a=free_shape[0])
        elif len(free_shape) == 3:
            v = v.rearrange("p (a b c) -> p a b c", a=free_shape[0], b=free_shape[1])
        return v

    def mark(self):
        return self.cur

    def reset(self, m):
        self.cur = m


class K:
    def __init__(self, nc, stack):
        self.nc = nc
        self.q = {e: [] for e in ("pe", "act", "dve", "pool", "sp")}
        self.sem = {}
        self.cnt = {}
        for e in ("pe", "act", "dve"):
            self.sem[e] = stack.enter_context(nc.semaphore("s_" + e))
            self.cnt[e] = 0
        self.stack = stack
        self.nsem = 0

    def newsem(self, name):
        self.nsem += 1
        return self.stack.enter_context(self.nc.semaphore(f"{name}_{self.nsem}"))

    def op(self, eng, fn, deps=(), sig=True):
        deps = [d for d in deps if d is not None]
        tok = None
        if sig:
            self.cnt[eng] += 1
            tok = Tok(self.sem[eng], self.cnt[eng])
        self.q[eng].append((fn, deps, tok, 1))
        return tok

    def dma(self, queue, fn, sem, val, deps=()):
        deps = [d for d in deps if d is not None]
        tok = Tok(sem, val)
        self.q[queue].append((fn, deps, tok, 16))
        return tok

    def emit(self, eng, e):
        seen = {}
        for fn, deps, tok, inc in self.q[eng]:
            for d in deps:
                key = id(d.sem)
                if seen.get(key, -1) >= d.val:
                    continue
                seen[key] = d.val
                e.wait_ge(d.sem, d.val)
            ins = fn(e)
            if tok is not None:
                ins.then_inc(tok.sem, inc)

    def final_wait(self, eng, toks):
        self.q[eng].append((None, toks, None, 0))


class DmaSlot:
    def __init__(self, k, name):
        self.sem = k.newsem(name)
        self.n = 0

    def next(self):
        self.n += 1
        return self.sem, 16 * self.n


def build(cfg, stop=99, sub=99):
    nc = bass.Bass("TRN2", target_bir_lowering=False)
    from contextlib import ExitStack
    c = cfg
    D, NKC, HQ, HKV, DC, NCC, NT, TT, TQ, NMC = c.D, c.NKC, c.HQ, c.HKV, c.DC, c.NCC, c.NT, c.TT, c.TQ, c.NMC
    NTILE = NT + 2

    def din(name, shape, dt=F32):
        return nc.dram_tensor(name, list(shape), dt, kind="ExternalInput").ap()

    def dout(name, shape):
        return nc.dram_tensor(name, list(shape), F32, kind="ExternalOutput").ap()

    x = din("x", [TT, D])
    csel = din("csel", [128, D])
    w_ada = din("w_ada", [D, 3 * D])
    b_ada = din("b_ada", [3 * D])
    norm_g = din("norm_g", [D])
    w_in = din("w_in", [D, c.DIN])
    q_g = din("q_g", [128])
    k_g = din("k_g", [128])
    sinks = din("sinks", [HQ])
    conv_w = din("conv_w", [CONV_W, DC])
    conv_b = din("conv_b", [DC])
    ln_g = din("ln_g", [DC])
    ln_b = din("ln_b", [DC])
    w_pw2 = din("w_pw2", [DC, DC])
    w_out = din("w_out", [c.DMIX, D])
    ck = din("ck", [NSB, WINDOW, HKV, 128])
    cv = din("cv", [NSB, WINDOW, HKV, 128])
    st = din("st", [NSB, CONV_W - 1, DC])
    flag = din("flag", [128, 1])
    identf = din("identf", [128, 128])
    biasp = din("biasp", [HKV, 128, 2, 4, 128])
    biasp0 = din("biasp0", [HKV, 128, 4, 128])
    biasc = din("biasc", [HKV, 128, 4 * NS])
    biasn = din("biasn", [HKV, NS, 4 * NS])

    y_p = dout("y_p", [NT * 128, D])
    y_s = dout("y_s", [NS, D])
    k_p = dout("k_p", [128, c.KVW])
    v_p = dout("v_p", [128, c.KVW])
    conv_p = dout("conv_p", [CONV_W - 1, DC])
    k_so = dout("k_so", [NSB, WINDOW - 4, c.KVW])
    v_so = dout("v_so", [NSB, WINDOW - 4, c.KVW])
    conv_so = dout("conv_so", [NSB, CONV_W - 5, DC])
    k_sn = dout("k_sn", [NS, c.KVW])
    v_sn = dout("v_sn", [NS, c.KVW])
    conv_sn = dout("conv_sn", [NS, DC])
    gscr = nc.dram_tensor("gscr", [128, D], F32, kind="Internal").ap()

    stack = ExitStack()
    POOLB = 212800
    pool_t = stack.enter_context(nc.sbuf_tensor("pool", [128, POOLB], U8))
    psum_t = stack.enter_context(nc.psum_tensor("ps", [128, 8, 512], F32))
    k = K(nc, stack)
    ps = psum_t[:]
    ps_bf = ps.bitcast(BF16)

    A = Arena(pool_t, 0, POOLB)
    HT = A.alloc([NKC, TT], BF16)
    MIX = A.alloc([NMC, TQ], BF16)
    NSLOT = 2
    WR = [A.alloc([NKC, 128], BF16) for _ in range(NSLOT)]
    IDF = A.alloc([128], F32)
    IDB = A.alloc([128], BF16)
    ONESB = A.alloc([128], BF16)
    ONESF = A.alloc([128], F32)
    BT = A.alloc([3 * NKC], F32)
    NV = NKC + 3 * NCC + 2
    VEC = A.alloc([NV], F32)
    NG = VEC[:, 0:NKC]
    CB = VEC[:, NKC:NKC + NCC]
    LG = VEC[:, NKC + NCC:NKC + 2 * NCC]
    LB = VEC[:, NKC + 2 * NCC:NKC + 3 * NCC]
    QG = VEC[:, NKC + 3 * NCC:NKC + 3 * NCC + 1]
    KG = VEC[:, NKC + 3 * NCC + 1:NKC + 3 * NCC + 2]
    CW = A.alloc([NCC, CONV_W], F32)
    ESK = A.alloc([HQ], F32)
    FLG = A.alloc([1], F32)
    QS = A.alloc([HQ, NS], BF16)
    KSN = A.alloc([HKV, NS], BF16)
    VSN = A.alloc([HKV, 128], BF16, parts=NS)
    SMALL = A.alloc([16], F32)
    base_mark = A.mark()

    wslots = [DmaSlot(k, "w") for _ in range(NSLOT)]
    wfree = [None] * NSLOT
    wstate = {"i": 0}
    out_toks = []

    def chunks(t0, t1):
        r, s = [], t0
        while s < t1:
            n = min(512, t1 - s)
            r.append((s, n))
            s += n
        return r

    bank_free = [None] * 8
    setstate = {"i": 0}

    def next_set():
        s = setstate["i"]
        setstate["i"] ^= 1
        return [0, 1, 2] if s == 0 else [3, 4, 5]

    miscstate = {"i": 0}

    def next_misc():
        b = 6 + miscstate["i"]
        miscstate["i"] ^= 1
        return b

    def free_deps(banks):
        return [bank_free[b] for b in banks]

    def set_free(banks, tok):
        for b in banks:
            bank_free[b] = tok

    pending = []

    tickers = []

    def flush_pending():
        todo = list(pending)
        del pending[:]
        for f in todo:
            f()
        for t in tickers:
            t()

    def issue_weight(desc):
        src, nkc = desc
        s = wstate["i"] % NSLOT
        wstate["i"] += 1
        sem, val = wslots[s].next()
        tok = k.dma("pool", lambda e, s=s, src=src, nkc=nkc: e.dma_start(out=WR[s][:, 0:nkc, :], in_=src),
                    sem, val, deps=[wfree[s]])
        return s, tok

    def run_blocks(all_blocks):
        LOOK = NSLOT - 1
        ring = [b for b in all_blocks if "own" not in b]
        own = [b for b in all_blocks if "own" in b]
        issued = {}
        for j in range(min(LOOK, len(ring))):
            issued[id(ring[j])] = issue_weight(ring[j]["w"])
        ring_pos = {id(b): i for i, b in enumerate(ring)}
        own_pos = {id(b): i for i, b in enumerate(own)}

        def issue_own(b):
            buf, dslot, st_ = b["own"]
            src, nkc_ = b["w"]
            sem, val = dslot.next()
            tok_ = k.dma("pool", lambda e: e.dma_start(out=buf[:, 0:nkc_, :], in_=src), sem, val,
                         deps=[st_.get("free")] + list(b.get("own_deps", [])))
            issued[id(b)] = (buf, tok_)
        if own:
            issue_own(own[0])
        for b in all_blocks:
            if "own" in b:
                wbuf, wtok = issued[id(b)]
                slot = None
            else:
                j = ring_pos[id(b)]
                if j + LOOK < len(ring):
                    issued[id(ring[j + LOOK])] = issue_weight(ring[j + LOOK]["w"])
                slot, wtok = issued[id(b)]
                wbuf = WR[slot]
            nkc = b["w"][1]
            banks = next_set()
            ch = chunks(b["t0"], b["t1"])
            deps = [wtok] + free_deps(banks[:len(ch)]) + list(b.get("deps", []))
            last = None
            for kc in range(nkc):
                for ci, (c0, n) in enumerate(ch):
                    islast = (kc == nkc - 1 and ci == len(ch) - 1)
                    last = k.op("pe", lambda e, kc=kc, ci=ci, c0=c0, n=n, wbuf=wbuf, banks=banks, b=b, nkc=nkc:
                                e.matmul(ps[:, banks[ci], 0:n], wbuf[:, kc, :], b["mov"](kc, c0, n),
                                         start=(kc == 0), stop=(kc == nkc - 1)),
                                deps=deps if (kc == 0 and ci == 0) else (), sig=islast)
                if nkc >= 9 and "own" not in b and kc in (nkc // 3 - 1, (2 * nkc) // 3 - 1):
                    flush_pending()
                elif nkc == 8 and "own" not in b and kc == 3:
                    flush_pending()
            if "own" in b:
                b["own"][2]["free"] = last
                b["own"][2]["last_pe"] = last
                i_ = own_pos[id(b)]
                if i_ + 1 < len(own):
                    issue_own(own[i_ + 1])
            else:
                wfree[slot] = last
            flush_pending()
            ftok = b["handler"](banks, ch, last)
            set_free(banks[:len(ch)], ftok)
        flush_pending()
        flush_pending()

    def psv(banks, ch):
        return [(ps[:, banks[i], 0:n], c0, n) for i, (c0, n) in enumerate(ch)]

    def act_chunks(banks, ch, tok, outfn, func, **kw):
        t = None
        for (p, c0, n) in psv(banks, ch):
            t = k.op("act", lambda e, p=p, c0=c0, n=n: e.activation(outfn(c0, n), p, func, **kw), deps=[tok])
        return t

    def dve_copy_chunks(banks, ch, tok, outfn):
        t = None
        for (p, c0, n) in psv(banks, ch):
            t = k.op("dve", lambda e, p=p, c0=c0, n=n: e.tensor_copy(outfn(c0, n), p), deps=[tok])
        return t

    gscr_toks = []

    def finish():
        k.final_wait("sp", [t for t in out_toks if t is not None] + gscr_toks)

        with nc.Block() as block:
            def mk(engname):
                def f(e):
                    items = k.q[engname]
                    seen = {}
                    for fn, deps, tok, inc in items:
                        for d in deps:
                            key = id(d.sem)
                            if seen.get(key, -1) >= d.val:
                                continue
                            seen[key] = d.val
                            e.wait_ge(d.sem, d.val)
                        if fn is None:
                            continue
                        ins = fn(e)
                        if tok is not None:
                            ins.then_inc(tok.sem, inc)
                return f
            block.tensor(mk("pe"))
            block.scalar(mk("act"))
            block.vector(mk("dve"))
            block.gpsimd(mk("pool"))
            block.sync(mk("sp"))

        stack.close()
        return nc

    css = [DmaSlot(k, f"const{i}") for i in range(6)]
    cst_i = {"i": 0, "last": [None] * 6}
    ctoks = []

    def cload(out_ap, in_ap, slow=False, deps=()):
        i = cst_i["i"] % 6
        cst_i["i"] += 1
        sem, val = css[i].next()
        t = k.dma("sp", lambda e: e.dma_start(out=out_ap, in_=in_ap, allow_slow_non_contiguous=slow), sem, val,
                  deps=[cst_i["last"][i]] + list(deps))
        cst_i["last"][i] = t
        ctoks.append(t)
        return t

    T_IDF = cload(IDF, identf)
    t_esk = cload(ESK, sinks.partition_broadcast(128))

    def const_loads_late(RW1, RW2, CWS, t_cwz):
        toks = []
        toks.append(cload(RW1[0:3 * NKC, :], b_ada.rearrange("(a p) -> a p", p=128)))
        r = 0
        for (src, n_) in ((norm_g, NKC), (conv_b, NCC), (ln_g, NCC), (ln_b, NCC), (q_g, 1), (k_g, 1)):
            toks.append(cload(RW2[r:r + n_, :], src.rearrange("(a p) -> a p", p=128)))
            r += n_
        toks.append(cload(CWS, conv_w, deps=[t_cwz]))
        toks.append(cload(FLG, flag))
        return toks
    osem = DmaSlot(k, "oldrows")
    for (dst, src) in ((k_so, ck.rearrange("b j n d -> b j (n d)")[:, 4:WINDOW, :]),
                       (v_so, cv.rearrange("b j n d -> b j (n d)")[:, 4:WINDOW, :]),
                       (conv_so, st[:, 4:CONV_W - 1, :])):
        sem, val = osem.next()
        out_toks.append(k.dma("sp", lambda e, dst=dst, src=src: e.dma_start(out=dst, in_=src), sem, val,
                              deps=[out_toks[-1]] if out_toks else []))

    T_SMALL = k.op("dve", lambda e: e.memset(SMALL, 0.0))
    T_SMALL = k.op("dve", lambda e: e.memset(SMALL[:, 9:10], RMS_EPS), deps=[T_SMALL])
    t_c1 = k.op("dve", lambda e: e.tensor_copy(IDB, IDF), deps=[T_IDF])
    t_c2 = k.op("dve", lambda e: e.memset(ONESB, 1.0))
    t_c3 = k.op("dve", lambda e: e.memset(ONESF, 1.0))
    t_c4 = k.op("act", lambda e: e.activation(ESK, ESK, AF.Exp), deps=[t_esk])

    jstate = {"last": None}

    def join(deps):
        t = k.op("act", lambda e: e.activation(SMALL[:, 8:9], SMALL[:, 8:9], AF.Copy),
                 deps=list(deps) + [T_SMALL, jstate["last"]])
        jstate["last"] = t
        return t

    if stop <= 0:
        return finish()
    NSL = 66
    CTT = A.alloc([NKC, NSL], BF16)
    GTB = [A.alloc([NSL], F32) for _ in range(2)]
    GOB = [A.alloc([128], F32) for _ in range(2)]
    m0 = A.mark()
    GALL = A.alloc([NKC, NSL], F32)
    SHALL = A.alloc([NKC, NSL], F32)
    m1 = A.mark()
    mix_lo = NKC * TT * 2
    mix_lo = (mix_lo + 63) // 64 * 64
    MOV = Arena(pool_t, mix_lo, mix_lo + NMC * TQ * 2)
    SLB = 4 if NKC % 4 == 0 and D >= 512 else 1
    big01 = MOV if (NMC * TQ * 2) >= max(2 * NKC * SLB * 128 * 2, 6 * D * 2) + 4096 else A
    if big01 is MOV and NKC * SLB * 128 * 2 >= 6 * D + 128:
        w0 = big01.alloc([NKC, SLB * 128], BF16)
        mk_ = big01.mark()
        CIN = big01.alloc([D], F32)
        CSL = big01.alloc([D], BF16)
        big01.reset(mk_)
        WSL = [w0, big01.alloc([NKC, SLB * 128], BF16)]
        wsl1_alias = True
    else:
        CIN = A.alloc([D], F32)
        CSL = A.alloc([D], BF16)
        WSL = [big01.alloc([NKC, SLB * 128], BF16) for _ in range(2)]
        wsl1_alias = False
    cin_s = DmaSlot(k, "cin")
    sem, val = cin_s.next()
    t_cin = k.dma("sp", lambda e: e.dma_start(out=CIN, in_=csel), sem, val)
    wsl = [DmaSlot(k, "wsl0"), DmaSlot(k, "wsl1")]
    wav = w_ada.rearrange("(a p) n -> p a n", p=128)
    NSLB = 2 * NKC // SLB
    wsl_tok = {}
    wsl_free = [None, None]

    def issue_wsl(j):
        sl = j % 2
        sem, val = wsl[sl].next()
        wsl_tok[j] = k.dma("pool", lambda e: e.dma_start(out=WSL[sl], in_=wav[:, :, j * SLB * 128:(j + 1) * SLB * 128]),
                           sem, val, deps=[wsl_free[sl]])
    issue_wsl(0)
    RW1 = A.alloc([128], F32)
    RW2 = A.alloc([128], F32)
    CWS = A.alloc([DC], F32, parts=32)
    t_cwz = k.op("dve", lambda e: e.memset(CWS, 0.0))
    ltoks = const_loads_late(RW1, RW2, CWS[0:CONV_W, :], t_cwz)
    bank = next_misc()
    k.op("pe", lambda e, bank=bank: e.transpose(ps[:, bank, 0:3 * NKC], RW1[0:3 * NKC, :], IDF[0:3 * NKC, 0:3 * NKC]),
         deps=ltoks + [T_IDF, bank_free[bank]], sig=False)
    tp_ = k.op("pe", lambda e, bank=bank: e.transpose(ps[:, bank, 128:128 + NV], RW2[0:NV, :], IDF[0:NV, 0:NV]))
    k.op("dve", lambda e, bank=bank: e.tensor_copy(BT, ps[:, bank, 0:3 * NKC]), deps=[tp_], sig=False)
    t_v = k.op("dve", lambda e, bank=bank: e.tensor_copy(VEC, ps[:, bank, 128:128 + NV]))
    bank_free[bank] = t_v
    bank = next_misc()
    tp_ = None
    for cc in range(NCC):
        tp_ = k.op("pe", lambda e, cc=cc, bank=bank: e.transpose(ps[:, bank, cc * 32:(cc + 1) * 32],
                                                                CWS[:, cc * 128:(cc + 1) * 128], IDF[0:32, 0:32]),
                   deps=ltoks + [T_IDF, bank_free[bank]] if cc == 0 else (), sig=(cc == NCC - 1))
    t_cw = k.op("dve", lambda e, bank=bank: e.tensor_copy(CW, ps[:, bank, 0:NCC * 32].rearrange("p (a b) -> p a b", b=32)[:, :, 0:CONV_W]),
                deps=[tp_])
    bank_free[bank] = t_cw
    t_c5 = k.op("dve", lambda e: e.tensor_scalar(BT[:, NKC:2 * NKC], BT[:, NKC:2 * NKC], 1.0, None, ALU.add), deps=[t_v])
    CONST_TOKS = [t_c1, t_c2, t_c3, t_c4, t_c5, t_cw, t_v, T_SMALL] + ltoks
    t_silu = k.op("act", lambda e: e.activation(CSL, CIN, AF.Silu), deps=[t_cin])
    t_ct = None
    for g0 in range(0, NKC, 8):
        bank = next_misc()
        g1 = min(NKC, g0 + 8)
        lastpe = None
        for kc in range(g0, g1):
            lastpe = k.op("pe", lambda e, kc=kc, bank=bank, g0=g0: e.transpose(
                ps_bf[:, bank, (kc - g0) * 128:(kc - g0 + 1) * 128], CSL[:, kc * 128:(kc + 1) * 128], IDB),
                deps=[t_silu, t_c1, bank_free[bank]] if kc == g0 else (), sig=(kc == g1 - 1))
        t_ct = k.op("dve", lambda e, bank=bank, g0=g0, g1=g1: e.tensor_copy(
            CTT[:, g0:g1, :], ps_bf[:, bank, 0:(g1 - g0) * 128].rearrange("p (a b) -> p a b", b=128)[:, :, 0:NSL]),
            deps=[lastpe])
        bank_free[bank] = t_ct

    if NSLB > 1:
        wsl_free[1] = t_ct
        issue_wsl(1)
    pb = {"i": 0}
    last_ev = None
    for j in range(NSLB):
        sl = j % 2
        lp = None
        for bi in range(SLB):
            q = j * SLB + bi
            part, kc_o = q // NKC, q % NKC
            bank = pb["i"] % 6
            pb["i"] += 1
            for kc in range(NKC):
                lp = k.op("pe", lambda e, kc=kc, sl=sl, bi=bi, bank=bank: e.matmul(
                    ps[:, bank, 0:NSL], WSL[sl][:, kc, bi * 128:(bi + 1) * 128], CTT[:, kc, :],
                    start=(kc == 0), stop=(kc == NKC - 1)),
                    deps=[wsl_tok[j], t_ct, bank_free[bank]] if kc == 0 else (), sig=(kc == NKC - 1))
            if part == 0:
                ev = k.op("dve", lambda e, kc_o=kc_o, bank=bank: e.tensor_scalar(
                    SHALL[:, kc_o, :], ps[:, bank, 0:NSL], BT[:, kc_o:kc_o + 1], None, ALU.add), deps=[lp] + CONST_TOKS)
            else:
                ev = k.op("dve", lambda e, kc_o=kc_o, bank=bank: e.tensor_scalar(
                    GALL[:, kc_o, :], ps[:, bank, 0:NSL], BT[:, NKC + kc_o:NKC + kc_o + 1], NG[:, kc_o:kc_o + 1],
                    ALU.add, ALU.mult), deps=[lp] + CONST_TOKS)
            bank_free[bank] = ev
            last_ev = ev
        wsl_free[sl] = lp
        if j + 2 < NSLB:
            issue_wsl(j + 2)

    gsl = [DmaSlot(k, "gscr0"), DmaSlot(k, "gscr1")]
    gstate = {"i": 0, "toks": [None, None], "dtoks": [None, None]}

    def h_gate(kc):
        def h(banks, ch, tok):
            i = gstate["i"] % 2
            gstate["i"] += 1
            t1 = k.op("act", lambda e: e.activation(GTB[i], ps[:, banks[0], 0:NSL], AF.Identity,
                                                    bias=BT[:, 2 * NKC + kc:2 * NKC + kc + 1]),
                      deps=[tok, gstate["toks"][i]] + CONST_TOKS)

            def later():
                bank = next_misc()
                t2 = k.op("pe", lambda e: e.transpose(ps[0:NSL, bank, 0:128], GTB[i], IDF),
                          deps=[t1, bank_free[bank]])
                gstate["toks"][i] = t2
                t3 = k.op("act", lambda e: e.activation(GOB[i][0:NSL, :], ps[0:NSL, bank, 0:128], AF.Copy),
                          deps=[t2, gstate["dtoks"][i]])
                bank_free[bank] = t3
                sem, val = gsl[i].next()
                t4 = k.dma("sp", lambda e: e.dma_start(out=gscr[0:NSL, kc * 128:(kc + 1) * 128], in_=GOB[i][0:NSL, :]),
                           sem, val, deps=[t3])
                gstate["dtoks"][i] = t4
                gscr_toks.append(t4)
            pending.append(later)
            return t1
        return h

    ngm = -(-(NKC * 128) // TQ)
    GW_OK = (HQ - ngm) >= 1
    gate_state = {"free": None}
    if GW_OK:
        gw_lo = mix_lo + (HQ - ngm) * TQ * 2
        GWB = pool_t[:, gw_lo:gw_lo + NKC * 128 * 2].bitcast(BF16).rearrange("p (a b) -> p a b", a=NKC)
        gw_slot = DmaSlot(k, "gatew")
    gate_blocks = []
    for kc in range(NKC):
        gb = dict(w=(wav[:, :, 2 * D + kc * 128:2 * D + (kc + 1) * 128], NKC),
                  mov=lambda kk, c0, n: CTT[:, kk, c0:c0 + n], t0=0, t1=NSL, deps=[t_ct], handler=h_gate(kc))
        if GW_OK:
            gb["own"] = (GWB, gw_slot, gate_state)
        gate_blocks.append(gb)
    T_MOD = last_ev
    PH0_DONE = [T_MOD, Tok(k.sem["pe"], k.cnt["pe"])]
    A.reset(m1)
    if big01 is MOV:
        MOV.reset(mix_lo)

    if stop <= 1:
        return finish()
    big1 = MOV if (NMC * TQ * 2) >= 6 * D * 2 + 4096 else A
    NXS = 3
    XS = [big1.alloc([D], F32) for _ in range(NXS)]
    XB = [big1.alloc([D], BF16) for _ in range(2)]
    TMPS = A.alloc([8, NS], F32)
    xsl = [DmaSlot(k, f"x{i}") for i in range(NXS)]
    xfree = [None] * NXS
    xbfree = [None, None]
    ht_tok = None
    grp = 8 if NKC >= 8 else NKC
    pbank = {"i": 0}
    def front(ti):
        rows = 128 if ti < NTILE - 1 else NS
        r0 = ti * 128
        s = ti % 2
        sx = ti % NXS
        sem, val = xsl[sx].next()
        t_x = k.dma("sp", lambda e, sx=sx, r0=r0, rows=rows: e.dma_start(out=XS[sx][0:rows, :], in_=x[r0:r0 + rows, :]),
                    sem, val, deps=[xfree[sx]] + PH0_DONE)
        ssq = SMALL[:, 2 * sx:2 * sx + 1]
        rs = SMALL[:, 2 * sx + 1:2 * sx + 2]
        t_sq = k.op("act", lambda e, s=s, sx=sx, rows=rows, ssq=ssq: e.activation(XB[s][0:rows, :], XS[sx][0:rows, :], AF.Square,
                                                                                accum_out=ssq[0:rows, :]),
                    deps=[t_x, xbfree[s]] + PH0_DONE)
        t_a = k.op("dve", lambda e, rows=rows, ssq=ssq, rs=rs: e.tensor_scalar(rs[0:rows, :], ssq[0:rows, :], 1.0 / D, RMS_EPS,
                                                                            ALU.mult, ALU.add), deps=[t_sq])
        t_b = k.op("act", lambda e, rows=rows, rs=rs: e.activation(rs[0:rows, :], rs[0:rows, :], AF.Ln), deps=[t_a])
        t_c = k.op("act", lambda e, rows=rows, rs=rs: e.activation(rs[0:rows, :], rs[0:rows, :], AF.Exp, scale=-0.5),
                   deps=[t_b])
        t_xb = k.op("dve", lambda e, s=s, sx=sx, rows=rows, rs=rs: e.tensor_scalar(XB[s][0:rows, :], XS[sx][0:rows, :],
                                                                                 rs[0:rows, :], None, ALU.mult),
                    deps=[t_c, t_sq])
        xfree[sx] = t_xb
        return rows, r0, s, t_xb

    fr_ctx = {0: front(0)}
    for ti in range(NTILE):
        if ti + 1 < NTILE:
            fr_ctx[ti + 1] = front(ti + 1)
        rows, r0, s, t_xb = fr_ctx.pop(ti)
        lastev = None
        if sub <= 1:
            continue
        for g0 in range(0, NKC, grp):
            bank = pbank["i"] % 8
            pbank["i"] += 1
            lastpe = None
            for kc in range(g0, g0 + grp):
                lastpe = k.op("pe", lambda e, kc=kc, bank=bank, g0=g0, s=s, rows=rows: e.transpose(
                    ps_bf[:, bank, (kc - g0) * 128:(kc - g0) * 128 + rows], XB[s][0:rows, kc * 128:(kc + 1) * 128],
                    IDB[0:rows, 0:rows]),
                    deps=[t_xb, bank_free[bank], t_c1] if kc == g0 else (), sig=(kc == g0 + grp - 1))
            if sub <= 2:
                bank_free[bank] = lastpe
                continue
            if ti < NTILE - 1:
                ev = None
                ev2 = None
                for kc in range(g0, g0 + grp):
                    src = ps_bf[:, bank, (kc - g0) * 128:(kc - g0 + 1) * 128]
                    dst = HT[:, kc, r0:r0 + 128]
                    if ((g0 // grp + ti) % 2 == 0 and sub != 4) or sub == 3:
                        ev = k.op("dve", lambda e, src=src, dst=dst, kc=kc: e.tensor_scalar(
                            dst, src, GALL[:, kc, 64:65], SHALL[:, kc, 64:65], ALU.mult, ALU.add), deps=[lastpe, T_MOD])
                    else:
                        ev2 = k.op("act", lambda e, src=src, dst=dst, kc=kc: e.activation(
                            dst, src, AF.Identity, bias=SHALL[:, kc, 64:65], scale=GALL[:, kc, 64:65]),
                            deps=[lastpe, T_MOD])
                evj = join([t for t in (ev, ev2) if t is not None])
                bank_free[bank] = evj
                lastev = evj
            elif sub in (3, 4):
                bank_free[bank] = lastpe
            else:
                pv = ps_bf[:, bank, 0:grp * 128].rearrange("p (a b) -> p a b", b=128)[:, :, 0:NS]
                tm = TMPS[:, 0:grp, :]
                e1 = k.op("dve", lambda e, pv=pv, tm=tm, g0=g0: e.tensor_tensor(tm, pv, GALL[:, g0:g0 + grp, 0:NS], ALU.mult),
                          deps=[lastpe, T_MOD, lastev])
                e2 = k.op("dve", lambda e, tm=tm, g0=g0, r0=r0: e.tensor_tensor(HT[:, g0:g0 + grp, r0:r0 + NS], tm,
                                                                             SHALL[:, g0:g0 + grp, 0:NS], ALU.add),
                          deps=[e1])
                bank_free[bank] = e2
                lastev = e2
        xbfree[s] = lastpe
        ht_tok = lastev
    HT_DONE = [Tok(k.sem["dve"], k.cnt["dve"]), Tok(k.sem["act"], k.cnt["act"])]
    A.reset(m0)

    if stop <= 2:
        return finish()
    wiv = w_in.rearrange("(a p) n -> p a n", p=128)

    def hmov(kk, c0, n):
        return HT[:, kk, c0:c0 + n]

    mconv = A.mark()
    A32 = A.alloc([TT], F32)
    SIG = A.alloc([TT], F32)
    NP = NT * 128
    ACCA = A.alloc([NP], F32)
    ACCB = A.alloc([NP], F32)
    ACC1 = A.alloc([TQ], F32)
    ACC2 = A.alloc([TQ], F32)
    UE = A.alloc([NSB, 34], F32)
    YSA = A.alloc([NSB, 4], F32)
    YSB = A.alloc([NSB, 4], F32)
    STG = [A.alloc([4, 128], F32, parts=120) for _ in range(2)]
    UO = [A.alloc([128], F32) for _ in range(2)]
    stsl = [DmaSlot(k, "st0"), DmaSlot(k, "st1")]
    uosl = [DmaSlot(k, "uo0"), DmaSlot(k, "uo1")]
    cst = {"a32": None, "u_done": None, "stfree": [None, None], "uofree": [None, None], "ue_free": None,
           "acc_free": None, "sig_free": None}
    stv = st.rearrange("(g b) r d -> (b r) g d", b=4)
    U0 = 128 - (CONV_W - 1)

    def h_cua(cc):
        def h(banks, ch, tok):
            t = None
            for (p, c0, n) in psv(banks, ch):
                t = k.op("act", lambda e, p=p, c0=c0, n=n: e.activation(A32[:, c0:c0 + n], p, AF.Copy),
                         deps=[tok, cst["u_done"], cst.get("conv_done"), cst.get("un_done")])
            cst["a32"] = t
            return t
        return h

    def run_part_b():
        if cst.get("partB") is not None:
            f = cst["partB"]
            cst["partB"] = None
            f()

    def h_cub(cc):
        def h(banks, ch, tok):
            s = cc % 2
            flush_pending()
            run_part_b()
            sem, val = stsl[s].next()
            t_st = k.dma("sp", lambda e: e.dma_start(out=STG[s], in_=stv[:, :, cc * 128:(cc + 1) * 128]), sem, val,
                         deps=[cst["stfree"][s]] + HT_DONE)
            t_sig = None
            for (p, c0, n) in psv(banks, ch):
                t_sig = k.op("act", lambda e, p=p, c0=c0, n=n: e.activation(SIG[:, c0:c0 + n], p, AF.Sigmoid),
                             deps=[tok, cst["sig_free"]])
            t_u = k.op("dve", lambda e: e.tensor_tensor(A32, A32, SIG, ALU.mult), deps=[t_sig, cst["a32"]])
            t_u = k.op("dve", lambda e: e.tensor_scalar(A32[:, U0:128], A32[:, U0:128], FLG[:, 0:1], None, ALU.mult),
                       deps=[t_u] + CONST_TOKS)
            wv = lambda j: CW[:, cc, j:j + 1]
            ta = k.op("dve", lambda e: e.tensor_scalar(ACCA, A32[:, U0:U0 + NP], wv(0), CB[:, cc:cc + 1], ALU.mult, ALU.add),
                      deps=[t_u, cst["acc_free"]])
            tb = k.op("dve", lambda e: e.tensor_scalar(ACCB, A32[:, U0 + 1:U0 + 1 + NP], wv(1), None, ALU.mult),
                      deps=[t_u, cst["acc_free"]])
            for j in range(2, CONV_W):
                if j % 2 == 0:
                    ta = k.op("dve", lambda e, j=j: e.scalar_tensor_tensor(ACCA, A32[:, U0 + j:U0 + j + NP], wv(j), ACCA,
                                                                          ALU.mult, ALU.add), deps=[ta])
                else:
                    tb = k.op("dve", lambda e, j=j: e.scalar_tensor_tensor(ACCB, A32[:, U0 + j:U0 + j + NP], wv(j), ACCB,
                                                                          ALU.mult, ALU.add), deps=[tb])
            t_y = k.op("dve", lambda e: e.tensor_tensor(ACCA, ACCA, ACCB, ALU.add), deps=[ta, tb])
            cst["conv_done"] = t_y

            def later():
                bank = next_misc()
                lp = None
                for g in range(4):
                    lp = k.op("pe", lambda e, g=g: e.transpose(ps[:, bank, g * 120:(g + 1) * 120], STG[s][:, g, :],
                                                               IDF[0:120, 0:120]),
                              deps=[t_st, bank_free[bank]] + CONST_TOKS if g == 0 else (), sig=(g == 3))
                cst["stfree"][s] = lp
                t_ue = k.op("act", lambda e: e.activation(UE[:, :, 0:30], ps[:, bank, 0:480].rearrange("p (b r) -> p b r", r=30),
                                                          AF.Copy), deps=[lp, cst["ue_free"]])
                bank_free[bank] = t_ue
                t_un = k.op("act", lambda e: e.activation(UE[:, :, 30:34], A32[:, TT - NS:TT].rearrange("p (b t) -> p b t", t=4),
                                                          AF.Copy), deps=[t_u, cst["ue_free"]])
                cst["un_done"] = t_un
                sa = k.op("dve", lambda e: e.tensor_scalar(YSA, UE[:, :, 0:4], wv(0), CB[:, cc:cc + 1], ALU.mult, ALU.add),
                          deps=[t_ue, t_un, cst.get("ys_free")])
                sb = k.op("dve", lambda e: e.tensor_scalar(YSB, UE[:, :, 1:5], wv(1), None, ALU.mult),
                          deps=[t_ue, t_un, cst.get("ys_free")])
                for j in range(2, CONV_W):
                    if j % 2 == 0:
                        sa = k.op("dve", lambda e, j=j: e.scalar_tensor_tensor(YSA, UE[:, :, j:j + 4], wv(j), YSA,
                                                                              ALU.mult, ALU.add), deps=[sa])
                    else:
                        sb = k.op("dve", lambda e, j=j: e.scalar_tensor_tensor(YSB, UE[:, :, j:j + 4], wv(j), YSB,
                                                                              ALU.mult, ALU.add), deps=[sb])
                t_ys = k.op("dve", lambda e: e.tensor_tensor(YSA, YSA, YSB, ALU.add), deps=[sa, sb])
                cst["ue_free"] = t_ys
                bank2 = next_misc()
                t_tr = k.op("pe", lambda e: e.transpose(ps[:, bank2, 0:128], A32[:, TT - 128:TT], IDF),
                            deps=[t_u, bank_free[bank2]])
                t_uo = k.op("act", lambda e: e.activation(UO[s], ps[:, bank2, 0:128], AF.Copy),
                            deps=[t_tr, cst["uofree"][s]])
                bank_free[bank2] = t_uo
                cst["u_done"] = Tok(k.sem["pe"], t_tr.val)
                sem, val = uosl[s].next()
                d1 = k.dma("sp", lambda e: e.dma_start(out=conv_p[:, cc * 128:(cc + 1) * 128], in_=UO[s][34:64, :]),
                           sem, val, deps=[t_uo])
                sem, val = uosl[s].next()
                d2 = k.dma("sp", lambda e: e.dma_start(out=conv_sn[:, cc * 128:(cc + 1) * 128], in_=UO[s][64:128, :]),
                           sem, val, deps=[t_uo, d1])
                cst["uofree"][s] = d2
                out_toks.append(d2)
                cst["ue_free"] = t_ys
                cst["partB"] = lambda: part_b(t_ys)
            pending.append(later)

            def part_b(t_ys):
                ysv = YSA.rearrange("p b t -> p (b t)")
                gdep = [gate_state.get("last_pe")] if (GW_OK and cc >= HQ - ngm) else []
                assert not (GW_OK and cc >= HQ - ngm and gate_blocks), "gate blocks must be done before the overlay is reused"
                t1 = k.op("act", lambda e: e.activation(MIX[:, cc, 0:NP], ACCA, AF.Copy), deps=[t_y] + gdep)
                t2 = k.op("act", lambda e: e.activation(MIX[:, cc, NP:TQ], ysv, AF.Copy), deps=[t_ys] + gdep)
                if cc == 0:
                    t3 = k.op("dve", lambda e: e.tensor_copy(ACC1[:, 0:NP], ACCA), deps=[t_y])
                    t4 = k.op("dve", lambda e: e.tensor_copy(ACC1[:, NP:TQ], ysv), deps=[t_ys])
                else:
                    t3 = k.op("dve", lambda e: e.tensor_tensor(ACC1[:, 0:NP], ACC1[:, 0:NP], ACCA, ALU.add), deps=[t_y])
                    t4 = k.op("dve", lambda e: e.tensor_tensor(ACC1[:, NP:TQ], ACC1[:, NP:TQ], ysv, ALU.add), deps=[t_ys])
                t5 = k.op("act", lambda e: e.activation(SIG[:, 0:NP], ACCA, AF.Square), deps=[t_y, t_u])
                t6 = k.op("act", lambda e: e.activation(SIG[:, NP:TQ], ysv, AF.Square), deps=[t_ys, t_u])
                if cc == 0:
                    t7 = k.op("dve", lambda e: e.tensor_copy(ACC2, SIG[:, 0:TQ]), deps=[t5, t6])
                else:
                    t7 = k.op("dve", lambda e: e.tensor_tensor(ACC2, ACC2, SIG[:, 0:TQ], ALU.add), deps=[t5, t6])
                cst["sig_free"] = t7
                cst["acc_free"] = join([Tok(k.sem["dve"], k.cnt["dve"]), t5, t6])
                cst["ys_free"] = cst["acc_free"]
            return t_sig
        return h

    def h_cg(cc):
        def h(banks, ch, tok):
            return act_chunks(banks, ch, tok, lambda c0, n: MIX[:, HQ + cc, c0 - 128:c0 - 128 + n], AF.Silu)
        return h

    blocks = []
    ncc_ok = max(1, min(NCC, HQ - ngm)) if GW_OK else NCC
    gpc = -(-NKC // ncc_ok)
    for gb in gate_blocks:
        gb["own_deps"] = HT_DONE
    for cc in range(NCC):
        ng_left = gpc
        for (off, hf, t0) in ((c.oca, h_cua, 0), (c.ocb, h_cub, 0), (c.ocg, h_cg, 128)):
            col = off + cc * 128
            blocks.append(dict(w=(wiv[:, :, col:col + 128], NKC), mov=hmov, t0=t0, t1=TT, deps=HT_DONE,
                               handler=hf(cc)))
            if gate_blocks and ng_left > 0:
                blocks.append(gate_blocks.pop(0))
                ng_left -= 1
        while gate_blocks and ng_left > 0:
            blocks.append(gate_blocks.pop(0))
            ng_left -= 1
    blocks.extend(gate_blocks)
    del gate_blocks[:]
    run_blocks(blocks)
    flush_pending()
    run_part_b()
    flush_pending()
    CONV_DONE = [Tok(k.sem["dve"], k.cnt["dve"]), Tok(k.sem["act"], k.cnt["act"])]

    if stop <= 3:
        return finish()
    MEAN = A32[:, 0:TQ]
    RSTD = SIG[:, 0:TQ]
    t_ln = None
    for (src, which) in ((ACC1, 0), (ACC2, 1)):
        for (c0, n) in chunks(0, TQ):
            bank = next_misc()
            tp = k.op("pe", lambda e, bank=bank, c0=c0, n=n, src=src: e.matmul(ps[:, bank, 0:n], ONESF, src[:, c0:c0 + n],
                                                                              start=True, stop=True),
                      deps=CONV_DONE + [bank_free[bank]] + CONST_TOKS)
            if which == 0:
                t_ln = k.op("dve", lambda e, bank=bank, c0=c0, n=n: e.tensor_scalar(MEAN[:, c0:c0 + n], ps[:, bank, 0:n],
                                                                                  1.0 / DC, None, ALU.mult), deps=[tp])
            else:
                t_ln = k.op("dve", lambda e, bank=bank, c0=c0, n=n: e.tensor_scalar(RSTD[:, c0:c0 + n], ps[:, bank, 0:n],
                                                                                  1.0 / DC, None, ALU.mult), deps=[tp])
            bank_free[bank] = t_ln
    TMPL = ACCA[:, 0:NP] if NP >= TQ else ACC1
    TMPL = ACC1
    t_ln = k.op("dve", lambda e: e.tensor_tensor(TMPL, MEAN, MEAN, ALU.mult), deps=[t_ln])
    t_ln = k.op("dve", lambda e: e.tensor_tensor(RSTD, RSTD, TMPL, ALU.subtract), deps=[t_ln])
    t_ln = k.op("dve", lambda e: e.tensor_scalar(RSTD, RSTD, 0.0, LN_EPS, ALU.max, ALU.add), deps=[t_ln])
    t_ln = k.op("act", lambda e: e.activation(RSTD, RSTD, AF.Ln), deps=[t_ln])
    t_ln = k.op("act", lambda e: e.activation(RSTD, RSTD, AF.Exp, scale=-0.5), deps=[t_ln])
    TL = [ACC1, ACC2]
    tl_free = [None, None]
    act_toks = []
    for cc in range(NCC):
        i = cc % 2
        a1 = k.op("dve", lambda e, cc=cc, i=i: e.tensor_tensor(TL[i], MIX[:, cc, :], MEAN, ALU.subtract),
                  deps=[t_ln, tl_free[i]])
        a2 = k.op("dve", lambda e, i=i: e.tensor_tensor(TL[i], TL[i], RSTD, ALU.mult), deps=[a1])
        a3 = k.op("act", lambda e, cc=cc, i=i: e.activation(MIX[:, cc, :], TL[i], AF.Silu, bias=LB[:, cc:cc + 1],
                                                            scale=LG[:, cc:cc + 1]), deps=[a2])
        tl_free[i] = a3
        act_toks.append(a3)
    ACT_DONE = [Tok(k.sem["act"], k.cnt["act"])]

    wpv = w_pw2.rearrange("(a p) n -> p a n", p=128)

    def h_pw2(cc):
        def h(banks, ch, tok):
            t = None
            for (p, c0, n) in psv(banks, ch):
                t = k.op("dve", lambda e, p=p, c0=c0, n=n: e.tensor_tensor(MIX[:, HQ + cc, c0:c0 + n], p,
                                                                         MIX[:, HQ + cc, c0:c0 + n], ALU.mult), deps=[tok])
            return t
        return h

    blocks = []
    for cc in range(NCC):
        blocks.append(dict(w=(wpv[:, :, cc * 128:(cc + 1) * 128], NCC), mov=lambda kk, c0, n: MIX[:, kk, c0:c0 + n],
                           t0=0, t1=TQ, deps=ACT_DONE, handler=h_pw2(cc)))
    run_blocks(blocks)
    PW2_DONE = [Tok(k.sem["pe"], k.cnt["pe"])]
    CONVPH_DONE = PW2_DONE + [t for t in cst["uofree"] if t is not None] + gscr_toks[-2:]
    A.reset(base_mark)

    if stop <= 4:
        return finish()
    RAW = A.alloc([TT], F32)
    RAWV = A.alloc([TT], BF16)
    RVO = A.alloc([192], F32)
    SQ = A.alloc([TT], BF16)
    KT2 = [A.alloc([TT], BF16) for _ in range(2)]
    VT2 = [A.alloc([NTILE, 128], BF16) for _ in range(2)]
    Q4 = A.alloc([4, TQ], BF16)
    KN32 = A.alloc([192], F32)
    KO = [A.alloc([128], F32) for _ in range(2)]
    EX = A.alloc([2, 512], BF16)
    DEN = A.alloc([512], F32)
    BP = A.alloc([2, 4, 128], F32)
    kosl = [DmaSlot(k, "ko0"), DmaSlot(k, "ko1")]
    bpsl = DmaSlot(k, "bp")
    ast = {"raw_free": None, "rawv_free": None, "sq_free": None, "kt_free": [None, None],
           "vt_free": [None, None], "q_free": None, "kofree": [None, None], "koi": 0, "kt": None, "vt": None,
           "q": [None] * 4, "ga": {}, "bp_free": None, "ex_free": None, "den_free": None,
           "kn_free": None, "rvo_free": None}
    att_units = []

    def ko_out(src_ps, rows, dst, deps):
        i = ast["koi"] % 2
        ast["koi"] += 1
        c1 = k.op("act", lambda e: e.activation(KO[i][0:rows, :], src_ps, AF.Copy), deps=list(deps) + [ast["kofree"][i]])
        sem, val = kosl[i].next()
        d1 = k.dma("sp", lambda e: e.dma_start(out=dst, in_=KO[i][0:rows, :]), sem, val, deps=[c1])
        ast["kofree"][i] = d1
        out_toks.append(d1)
        return c1

    def qk_block(banks, ch, tok, t0, final):
        t_raw = None
        t_sq = None
        for (p, c0, n) in psv(banks, ch):
            t_raw = k.op("dve", lambda e, p=p, c0=c0, n=n: e.tensor_copy(RAW[:, c0:c0 + n], p), deps=[tok, ast["raw_free"]])
            t_sq = k.op("act", lambda e, c0=c0, n=n: e.activation(SQ[:, c0:c0 + n], RAW[:, c0:c0 + n], AF.Square),
                        deps=[t_raw, ast["sq_free"]])

        def later():
            sb = banks
            lastp = None
            for ci, (c0, n) in enumerate(ch):
                lastp = k.op("pe", lambda e, ci=ci, c0=c0, n=n: e.matmul(ps[:, sb[ci], 0:n], ONESB, SQ[:, c0:c0 + n],
                                                                        start=True, stop=True),
                             deps=[t_sq] + free_deps(sb[:len(ch)]) + CONST_TOKS if ci == 0 else ())
            ast["sq_free"] = lastp
            t_r = None
            for ci, (c0, n) in enumerate(ch):
                t_r = k.op("act", lambda e, ci=ci, n=n: e.activation(ps[:, sb[ci], 0:n], ps[:, sb[ci], 0:n], AF.Ln,
                                                                     bias=SMALL[:, 9:10], scale=1.0 / 128), deps=[lastp])
            t_r2 = None
            for ci, (c0, n) in enumerate(ch):
                t_r2 = k.op("act", lambda e, ci=ci, n=n: e.activation(ps[:, sb[ci], 0:n], ps[:, sb[ci], 0:n], AF.Exp,
                                                                      scale=-0.5), deps=[t_r])
            rsv = [(ps[:, sb[ci], 0:n], c0, n) for ci, (c0, n) in enumerate(ch)]
            ftok = final(t_r2, t_raw, rsv)
            set_free(sb[:len(ch)], ftok)
        pending.append(later)
        return t_raw

    def h_k(n):
        def h(banks, ch, tok):
            sl = n % 2
            KT = KT2[sl]

            def final(t_r, t_raw, rsv):
                if ast["kt_free"][sl] is None and n >= 2:
                    drain_units()
                t1 = None
                for (rp, c0, nn) in rsv:
                    t1 = k.op("dve", lambda e, rp=rp, c0=c0, nn=nn: e.scalar_tensor_tensor(
                        KT[:, c0:c0 + nn], RAW[:, c0:c0 + nn], KG[:, 0:1], rp, ALU.mult, ALU.mult),
                        deps=[t_r, t_raw, ast["kt_free"][sl]] + CONST_TOKS)
                t2 = None
                for (rp, c0, nn) in rsv:
                    lo, hi = max(c0, TT - 192), c0 + nn
                    if hi > lo:
                        t2 = k.op("dve", lambda e, rp=rp, c0=c0, lo=lo, hi=hi: e.scalar_tensor_tensor(
                            KN32[:, lo - (TT - 192):hi - (TT - 192)], RAW[:, lo:hi], KG[:, 0:1], rp[:, lo - c0:hi - c0],
                            ALU.mult, ALU.mult), deps=[t_r, t_raw, ast["kn_free"]])
                t3 = k.op("dve", lambda e: e.tensor_copy(KSN[:, n, :], KT[:, TT - NS:TT]), deps=[t1])
                ast["raw_free"] = t2
                ast["kt"] = t3
                ast["kt_free"][sl] = None

                def later2():
                    bank = next_misc()
                    k.op("pe", lambda e: e.transpose(ps[:, bank, 0:128], KN32[:, 0:128], IDF),
                         deps=[t2, bank_free[bank]] + CONST_TOKS, sig=False)
                    p2 = k.op("pe", lambda e: e.transpose(ps[0:NS, bank, 128:256], KN32[:, 128:192], IDF))
                    ast["kn_free"] = p2
                    ko_out(ps[:, bank, 0:128], 128, k_p[:, n * 128:(n + 1) * 128], [p2])
                    c2 = ko_out(ps[0:NS, bank, 128:256], NS, k_sn[:, n * 128:(n + 1) * 128], [p2])
                    bank_free[bank] = c2
                pending.append(later2)
                return t2
            return qk_block(banks, ch, tok, 0, final)
        return h

    def h_v(n):
        def h(banks, ch, tok):
            sl = n % 2
            VT = VT2[sl]
            t_raw = None
            for (p, c0, n_) in psv(banks, ch):
                t_raw = k.op("dve", lambda e, p=p, c0=c0, n_=n_: e.tensor_copy(RAWV[:, c0:c0 + n_], p),
                             deps=[tok, ast["rawv_free"]])
            t_rvo = None
            for (p, c0, n_) in psv(banks, ch):
                lo, hi = max(c0, TT - 192), c0 + n_
                if hi > lo:
                    t_rvo = k.op("dve", lambda e, p=p, c0=c0, lo=lo, hi=hi: e.tensor_copy(
                        RVO[:, lo - (TT - 192):hi - (TT - 192)], p[:, lo - c0:hi - c0]),
                        deps=[tok, ast["rvo_free"]])
            rel = t_rvo

            def later():
                if ast["vt_free"][sl] is None and n >= 2:
                    drain_units()
                lastpe = None
                tv = None
                for g0 in range(0, NTILE, 8):
                    bank = next_misc()
                    g1 = min(NTILE, g0 + 8)
                    for ti in range(g0, g1):
                        rows = 128 if ti < NTILE - 1 else NS
                        lastpe = k.op("pe", lambda e, ti=ti, rows=rows, bank=bank, g0=g0: e.transpose(
                            ps_bf[0:rows, bank, (ti - g0) * 128:(ti - g0 + 1) * 128], RAWV[:, ti * 128:ti * 128 + rows], IDB),
                            deps=[t_raw, bank_free[bank], ast["vt_free"][sl]] + CONST_TOKS if ti == g0 else (),
                            sig=(ti == g1 - 1))
                    for ti in range(g0, g1):
                        rows = 128 if ti < NTILE - 1 else NS
                        src = ps_bf[0:rows, bank, (ti - g0) * 128:(ti - g0 + 1) * 128]
                        tv = k.op("act", lambda e, ti=ti, rows=rows, src=src: e.activation(VT[0:rows, ti, :], src, AF.Copy),
                                  deps=[lastpe])
                    bank_free[bank] = tv
                ast["rawv_free"] = lastpe
                c3 = k.op("dve", lambda e: e.tensor_copy(VSN[:, n, :], VT[0:NS, NTILE - 1, :]), deps=[tv])
                ast["vt"] = c3
                ast["vt_free"][sl] = None
                bank = next_misc()
                k.op("pe", lambda e: e.transpose(ps[:, bank, 0:128], RVO[:, 0:128], IDF),
                     deps=[t_rvo, bank_free[bank]] + CONST_TOKS, sig=False)
                p2 = k.op("pe", lambda e: e.transpose(ps[0:NS, bank, 128:256], RVO[:, 128:192], IDF))
                ast["rvo_free"] = p2
                ko_out(ps[:, bank, 0:128], 128, v_p[:, n * 128:(n + 1) * 128], [p2])
                c2 = ko_out(ps[0:NS, bank, 128:256], NS, v_sn[:, n * 128:(n + 1) * 128], [p2])
                bank_free[bank] = c2
            pending.append(later)
            return rel
        return h

    def h_q(h_idx):
        j = h_idx % 4

        def h(banks, ch, tok):
            def final(t_r, t_raw, rsv):
                if j == 0:
                    drain_units()
                t1 = None
                for (rp, c0, nn) in rsv:
                    t1 = k.op("dve", lambda e, rp=rp, c0=c0, nn=nn: e.scalar_tensor_tensor(
                        Q4[:, j, c0 - 128:c0 - 128 + nn], RAW[:, c0:c0 + nn], QG[:, 0:1], rp, ALU.mult, ALU.mult),
                        deps=[t_r, t_raw, ast["q_free"]] + CONST_TOKS)
                t2 = k.op("dve", lambda e: e.tensor_copy(QS[:, h_idx, :], Q4[:, j, TQ - NS:TQ]), deps=[t1])
                ast["raw_free"] = t1
                ast["q"][j] = t2
                if j == 3:
                    schedule_attention(h_idx // 4)
                return t1
            return qk_block(banks, ch, tok, 128, final)
        return h

    def h_ga(h_idx):
        def h(banks, ch, tok):
            t = act_chunks(banks, ch, tok, lambda c0, n: MIX[:, h_idx, c0 - 128:c0 - 128 + n], AF.Silu)
            ast["ga"][h_idx] = t
            return t
        return h

    def run_unit(n, i, sl, deps0, tb):
        KT, VT = KT2[sl], VT2[sl]
        lp = None
        for blk in range(2):
            bank = 6 + blk
            lp = k.op("pe", lambda e, blk=blk, bank=bank: e.matmul(
                ps[:, bank, :].rearrange("p (h q) -> p h q", h=4), KT[:, (i + blk) * 128:(i + blk + 1) * 128],
                Q4[:, :, i * 128:(i + 1) * 128], start=True, stop=True),
                deps=deps0 + [bank_free[6], bank_free[7]] if blk == 0 else ())
        t_t = None
        for blk in range(2):
            bias = BP[:, blk].rearrange("p h q -> p (h q)")
            t_t = k.op("dve", lambda e, blk=blk, bias=bias: e.scalar_tensor_tensor(
                ps[:, 6 + blk, :], ps[:, 6 + blk, :], ISD, bias, ALU.mult, ALU.add), deps=[lp, tb])
        t_e = k.op("act", lambda e: e.activation(EX, ps[:, 6:8, :], AF.Exp), deps=[t_t, ast["ex_free"]])
        if i == 0:
            t_e = k.op("dve", lambda e: e.tensor_scalar(EX[:, 0, :], EX[:, 0, :], FLG[:, 0:1], None, ALU.mult),
                       deps=[t_e] + CONST_TOKS)
        bank_free[6] = t_e
        bank_free[7] = t_e

        def stage2():
            k.op("pe", lambda e: e.matmul(ps[:, 6, :], ONESB, EX[:, 0, :], start=True, stop=False),
                 deps=[t_e, bank_free[6], bank_free[7]] + CONST_TOKS, sig=False)
            p1 = k.op("pe", lambda e: e.matmul(ps[:, 6, :], ONESB, EX[:, 1, :], start=False, stop=True))
            k.op("pe", lambda e: e.matmul(ps[:, 7, :], VT[:, i, :], EX[:, 0, :], start=True, stop=False), sig=False)
            p2 = k.op("pe", lambda e: e.matmul(ps[:, 7, :], VT[:, i + 1, :], EX[:, 1, :], start=False, stop=True))
            ast["ex_free"] = p2
            d1 = k.op("dve", lambda e: e.tensor_tensor(
                DEN.rearrange("p (h q) -> p h q", h=4), ps[:, 6, :].rearrange("p (h q) -> p h q", h=4),
                ESK[:, 4 * n:4 * n + 4].unsqueeze(2).to_broadcast([128, 4, 128]), ALU.add),
                deps=[p1, ast["den_free"], t_c4])
            d2 = k.op("act", lambda e: e.activation(DEN, DEN, AF.Ln), deps=[d1])
            d2 = k.op("act", lambda e: e.activation(DEN, DEN, AF.Exp, scale=-1.0), deps=[d2])
            d3 = k.op("dve", lambda e: e.tensor_tensor(DEN, ps[:, 7, :], DEN, ALU.mult), deps=[d2, p2])
            bank_free[6] = d3
            bank_free[7] = d3
            mv = MIX[:, 4 * n:4 * n + 4, i * 128:(i + 1) * 128]
            d4 = k.op("dve", lambda e: e.tensor_tensor(mv, DEN.rearrange("p (h q) -> p h q", h=4), mv, ALU.mult),
                      deps=[d3] + [ast["ga"][4 * n + jj] for jj in range(4)])
            ast["den_free"] = d4
            if i == NT - 1:
                ast["bp_free"] = d4
                pt = Tok(k.sem["pe"], p2.val)
                ast["kt_free"][sl] = pt
                ast["vt_free"][sl] = pt
                ast["q_free"] = pt
        return stage2

    def schedule_attention(n):
        sl = n % 2
        deps0 = [ast["kt"], ast["vt"]] + list(ast["q"])
        sem, val = bpsl.next()
        tb = k.dma("sp", lambda e: e.dma_start(out=BP, in_=biasp[n]), sem, val, deps=[ast["bp_free"]] + CONVPH_DONE)
        for i in range(NT):
            att_units.append((n, i, sl, deps0, tb))

    inflight = {"s2": None}

    def att_tick():
        if inflight["s2"] is not None:
            s2 = inflight["s2"]
            inflight["s2"] = None
            s2()
        elif att_units:
            u = att_units.pop(0)
            inflight["s2"] = run_unit(*u)

    tickers.append(att_tick)

    def drain_units():
        while att_units or inflight["s2"] is not None:
            att_tick()

    def with_unit(hf):
        return hf

    blocks = []
    for n in range(HKV):
        blocks.append(dict(w=(wiv[:, :, c.ok + n * 128:c.ok + (n + 1) * 128], NKC), mov=hmov, t0=0, t1=TT,
                           deps=HT_DONE + CONVPH_DONE, handler=with_unit(h_k(n))))
        blocks.append(dict(w=(wiv[:, :, c.ov + n * 128:c.ov + (n + 1) * 128], NKC), mov=hmov, t0=0, t1=TT,
                           deps=HT_DONE + CONVPH_DONE, handler=with_unit(h_v(n))))
        for j in range(4):
            hh = 4 * n + j
            blocks.append(dict(w=(wiv[:, :, c.oga + hh * 128:c.oga + (hh + 1) * 128], NKC), mov=hmov, t0=128, t1=TT,
                               deps=HT_DONE + CONVPH_DONE, handler=with_unit(h_ga(hh))))
        for j in range(4):
            hh = 4 * n + j
            blocks.append(dict(w=(wiv[:, :, c.oq + hh * 128:c.oq + (hh + 1) * 128], NKC), mov=hmov, t0=128, t1=TT,
                               deps=HT_DONE + CONVPH_DONE, handler=with_unit(h_q(hh))))
    run_blocks(blocks)
    while att_units or pending or inflight["s2"] is not None:
        flush_pending()
    del tickers[:]
    ATT_P_DONE = [Tok(k.sem["dve"], k.cnt["dve"]), Tok(k.sem["pe"], k.cnt["pe"]), Tok(k.sem["act"], k.cnt["act"])]
    ATT_P_DONE += [t for t in ast["kofree"] if t is not None]

    if stop <= 5:
        return finish()
    ht_bytes = NKC * TT * 2
    if ht_bytes >= 40000:
        B = Arena(pool_t, 0, ht_bytes)
    else:
        A.reset(base_mark)
        B = A
    CK = [B.alloc([NSB, 128], BF16) for _ in range(2)]
    CV = [B.alloc([NSB, 128], BF16) for _ in range(2)]
    KCT = [B.alloc([NSB, 128], BF16) for _ in range(2)]
    EC = [B.alloc([4 * NS], BF16) for _ in range(2)]
    EN = [B.alloc([4 * NS], BF16, parts=NS) for _ in range(2)]
    BC = [B.alloc([4 * NS], F32) for _ in range(2)]
    BN = [B.alloc([4 * NS], F32, parts=NS) for _ in range(2)]
    DN = [B.alloc([4 * NS], F32) for _ in range(2)]
    cksl = [DmaSlot(k, "ck0"), DmaSlot(k, "ck1")]
    cvsl = [DmaSlot(k, "cv0"), DmaSlot(k, "cv1")]
    bcsl = [DmaSlot(k, "bc0"), DmaSlot(k, "bc1")]
    bnsl = [DmaSlot(k, "bn0"), DmaSlot(k, "bn1")]
    pfree = [None, None]
    ckv = ck.rearrange("b j n d -> j b n d")
    cvv = cv.rearrange("b j n d -> j b n d")
    sctx = {}

    def stage_a(n):
        p = n % 2
        bT0, bT1, bSC, bSN = 4 * p, 4 * p + 1, 4 * p + 2, 4 * p + 3
        sem, val = cksl[p].next()
        t_ck = k.dma("pool", lambda e: e.dma_start(out=CK[p], in_=ckv[:, :, n, :]), sem, val, deps=ATT_P_DONE + [pfree[p]])
        sem, val = cvsl[p].next()
        t_cv = k.dma("pool", lambda e: e.dma_start(out=CV[p], in_=cvv[:, :, n, :]), sem, val, deps=ATT_P_DONE + [pfree[p]])
        sem, val = bcsl[p].next()
        t_bc = k.dma("sp", lambda e: e.dma_start(out=BC[p], in_=biasc[n]), sem, val, deps=ATT_P_DONE + [pfree[p]])
        sem, val = bnsl[p].next()
        t_bn = k.dma("sp", lambda e: e.dma_start(out=BN[p], in_=biasn[n]), sem, val, deps=ATT_P_DONE + [pfree[p]])
        t_kct = None
        for gi_, g0 in enumerate(range(0, NSB, 8)):
            bank = (bT0, bT1)[gi_]
            lp = None
            for b_ in range(g0, g0 + 8):
                lp = k.op("pe", lambda e, b_=b_, bank=bank, g0=g0: e.transpose(
                    ps_bf[:, bank, (b_ - g0) * 128:(b_ - g0 + 1) * 128], CK[p][:, b_, :], IDB),
                    deps=[t_ck, bank_free[bank], pfree[p]] + ATT_P_DONE if b_ == g0 else (), sig=(b_ == g0 + 7))
            t_kct = k.op("act", lambda e, bank=bank, g0=g0: e.activation(
                KCT[p][:, g0:g0 + 8, :], ps_bf[:, bank, :].rearrange("p (a b) -> p a b", b=128), AF.Copy), deps=[lp])
            bank_free[bank] = t_kct
        lp = None
        for b_ in range(NSB):
            lp = k.op("pe", lambda e, b_=b_: e.matmul(ps[:, bSC, 16 * b_:16 * b_ + 16], KCT[p][:, b_, :],
                                                      QS[:, 4 * n:4 * n + 4, 4 * b_:4 * b_ + 4],
                                                      start=(b_ == 0), stop=(b_ == NSB - 1)),
                      deps=[t_kct, bank_free[bSC], bank_free[bSN]] + ATT_P_DONE if b_ == 0 else (), sig=False)
        lp = k.op("pe", lambda e: e.matmul(ps[0:NS, bSN, 0:4 * NS], KSN[:, n, :],
                                           QS[:, 4 * n:4 * n + 4, :].rearrange("p h (b t) -> p b h t", t=4),
                                           start=True, stop=True))
        t1 = k.op("dve", lambda e: e.scalar_tensor_tensor(ps[:, bSC, 0:4 * NS], ps[:, bSC, 0:4 * NS], ISD, BC[p],
                                                          ALU.mult, ALU.add), deps=[lp, t_bc])
        t2 = k.op("dve", lambda e: e.scalar_tensor_tensor(ps[0:NS, bSN, 0:4 * NS], ps[0:NS, bSN, 0:4 * NS], ISD, BN[p],
                                                          ALU.mult, ALU.add), deps=[lp, t_bn])
        e1 = k.op("act", lambda e: e.activation(EC[p], ps[:, bSC, 0:4 * NS], AF.Exp), deps=[t1, t2])
        e2 = k.op("act", lambda e: e.activation(EN[p], ps[0:NS, bSN, 0:4 * NS], AF.Exp), deps=[e1])
        bank_free[bSC] = e2
        bank_free[bSN] = e2
        sctx[n] = (e2, t_cv)

    def stage_b(n):
        p = n % 2
        bT0, bT1 = 4 * p, 4 * p + 1
        e2, t_cv = sctx.pop(n)
        k.op("pe", lambda e: e.matmul(ps[:, bT0, 0:4 * NS], ONESB, EC[p], start=True, stop=False),
             deps=[e2, bank_free[bT0], bank_free[bT1]], sig=False)
        p1 = k.op("pe", lambda e: e.matmul(ps[:, bT0, 0:4 * NS], ONESB[0:NS, :], EN[p], start=False, stop=True))
        k.op("pe", lambda e: e.matmul(ps[:, bT1, 0:4 * NS], VSN[:, n, :], EN[p], start=True, stop=False),
             deps=[t_cv], sig=False)
        p2 = None
        for b_ in range(NSB):
            p2 = k.op("pe", lambda e, b_=b_: e.matmul(ps[:, bT1, 16 * b_:16 * b_ + 16], CV[p][:, b_, :],
                                                      EC[p][:, 16 * b_:16 * b_ + 16], start=False, stop=(b_ == NSB - 1)),
                      sig=(b_ == NSB - 1))
        d1 = k.op("dve", lambda e: e.tensor_tensor(
            DN[p].rearrange("p (b h t) -> p b h t", b=NSB, h=4), ps[:, bT0, 0:4 * NS].rearrange("p (b h t) -> p b h t", b=NSB, h=4),
            ESK[:, 4 * n:4 * n + 4].unsqueeze(1).unsqueeze(3).to_broadcast([128, NSB, 4, 4]), ALU.add), deps=[p1, t_c4])
        d2 = k.op("act", lambda e: e.activation(DN[p], DN[p], AF.Ln), deps=[d1])
        d2 = k.op("act", lambda e: e.activation(DN[p], DN[p], AF.Exp, scale=-1.0), deps=[d2])
        d3 = k.op("dve", lambda e: e.tensor_tensor(DN[p], ps[:, bT1, 0:4 * NS], DN[p], ALU.mult), deps=[d2, p2])
        bank_free[bT0] = d3
        bank_free[bT1] = d3
        mv = MIX[:, 4 * n:4 * n + 4, TQ - NS:TQ].rearrange("p h (b t) -> p h b t", t=4)
        d4 = k.op("dve", lambda e: e.tensor_tensor(mv, DN[p].rearrange("p (b h t) -> p h b t", b=NSB, h=4), mv, ALU.mult),
                  deps=[d3])
        pfree[p] = join([d4, p2])

    order = []
    for n in range(HKV):
        order.append(("a", n))
        if n >= 1:
            order.append(("b", n - 1))
    order.append(("b", HKV - 1))
    for (what, n) in order:
        (stage_a if what == "a" else stage_b)(n)
    MIX_DONE = [Tok(k.sem["dve"], k.cnt["dve"]), Tok(k.sem["pe"], k.cnt["pe"]), Tok(k.sem["act"], k.cnt["act"])]

    if stop <= 6:
        return finish()
    OW = c.OW
    NOS = D // OW
    ov = Arena(pool_t, 0, ht_bytes)
    fr = Arena(pool_t, base_mark, POOLB)

    def alloc_any(shape, dt, parts=128):
        esz = {F32: 4, BF16: 2}[dt]
        nb = int(np.prod(shape)) * esz
        for ar in (ov, fr):
            st_ = (ar.cur + 63) // 64 * 64
            if st_ + nb <= ar.hi:
                return ar.alloc(shape, dt, parts)
        raise AssertionError("no room for w_out phase buffers")

    NXR = 3
    early = ht_bytes >= 40000 and B.peak <= NMC * OW * 2 and 2 * NMC * OW * 2 + 2 * NXR * OW * 4 <= ht_bytes
    if early:
        wo1 = ov.alloc([NMC, OW], BF16)
        wo0 = ov.alloc([NMC, OW], BF16)
        WO = [wo0, wo1]
        XR = [ov.alloc([OW], F32) for _ in range(NXR)]
        YO = [ov.alloc([OW], F32) for _ in range(NXR)]
        GP = fr.alloc([D], F32)
        GS_ = fr.alloc([D], F32, parts=NS)
        PRE = ATT_P_DONE
    else:
        WO = [alloc_any([NMC, OW], BF16) for _ in range(2)]
        GP = alloc_any([D], F32)
        GS_ = alloc_any([D], F32, parts=NS)
        XR = [alloc_any([OW], F32) for _ in range(NXR)]
        YO = [alloc_any([OW], F32) for _ in range(NXR)]
        PRE = MIX_DONE
    wosl = [DmaSlot(k, "wo0"), DmaSlot(k, "wo1")]
    xrsl = [DmaSlot(k, f"xr{i}") for i in range(NXR)]
    yosl = [DmaSlot(k, f"yo{i}") for i in range(NXR)]
    gpsl = DmaSlot(k, "gp")
    sem, val = gpsl.next()
    t_gp = k.dma("sp", lambda e: e.dma_start(out=GP, in_=gscr[64, :].partition_broadcast(128)),
                 sem, val, deps=gscr_toks[-2:] + PRE)
    sem, val = gpsl.next()
    t_gs = k.dma("sp", lambda e: e.dma_start(out=GS_, in_=gscr[0:NS, :]), sem, val, deps=[t_gp])
    wov = w_out.rearrange("(a p) n -> p a n", p=128)
    wofree = [None, None]
    xrfree = [None] * NXR
    yofree = [None] * NXR
    wo_toks = {}

    def issue_wo(sidx):
        s = sidx % 2
        sem, val = wosl[s].next()
        wo_toks[sidx] = k.dma("pool", lambda e: e.dma_start(out=WO[s], in_=wov[:, :, sidx * OW:(sidx + 1) * OW]), sem, val,
                              deps=(PRE if sidx == 0 else MIX_DONE) + [wofree[s]])
    issue_wo(0)
    it = 0
    bank_rr = 0
    for sidx in range(NOS):
        if sidx + 1 < NOS:
            issue_wo(sidx + 1)
        s = sidx % 2
        lastpe = None
        for ti in range(NT + 1):
            rows = 128 if ti < NT else NS
            m0_ = ti * 128
            r = it % NXR
            it += 1
            bank = bank_rr % 8
            bank_rr += 1
            sem, val = xrsl[r].next()
            t_xr = k.dma("sp", lambda e, r=r, rows=rows, m0_=m0_, sidx=sidx: e.dma_start(
                out=XR[r][0:rows, :], in_=x[128 + m0_:128 + m0_ + rows, sidx * OW:(sidx + 1) * OW]), sem, val,
                deps=[xrfree[r], t_gs])
            for kc in range(NMC):
                lastpe = k.op("pe", lambda e, kc=kc, bank=bank, rows=rows, m0_=m0_, s=s: e.matmul(
                    ps[0:rows, bank, 0:OW], MIX[:, kc, m0_:m0_ + rows], WO[s][:, kc, :], start=(kc == 0), stop=(kc == NMC - 1)),
                    deps=[wo_toks[sidx], bank_free[bank]] + MIX_DONE if kc == 0 else (), sig=(kc == NMC - 1))
            gate = (GP[0:rows, sidx * OW:(sidx + 1) * OW] if ti < NT else GS_[0:rows, sidx * OW:(sidx + 1) * OW])
            o1 = k.op("dve", lambda e, r=r, rows=rows, bank=bank, gate=gate: e.tensor_tensor(
                YO[r][0:rows, :], ps[0:rows, bank, 0:OW], gate, ALU.mult), deps=[lastpe, t_gs, yofree[r]])
            bank_free[bank] = o1
            o2 = k.op("dve", lambda e, r=r, rows=rows: e.tensor_tensor(YO[r][0:rows, :], YO[r][0:rows, :], XR[r][0:rows, :],
                                                                     ALU.add), deps=[o1, t_xr])
            xrfree[r] = o2
            dst = (y_p[m0_:m0_ + rows, sidx * OW:(sidx + 1) * OW] if ti < NT else y_s[:, sidx * OW:(sidx + 1) * OW])
            sem, val = yosl[r].next()
            d = k.dma("sp", lambda e, r=r, rows=rows, dst=dst: e.dma_start(out=dst, in_=YO[r][0:rows, :]), sem, val, deps=[o2])
            yofree[r] = d
            out_toks.append(d)
        wofree[s] = lastpe

    return finish()


def alibi_slopes(HQ):
    return (2.0 ** (-8.0 * np.arange(1, HQ + 1) / HQ)).astype(np.float32)


def const_tables(cfg, first_chunk):
    HQ, HKV = cfg.HQ, cfg.HKV
    sl = alibi_slopes(HQ).reshape(HKV, 4)
    kj = np.arange(128)[:, None].astype(np.float32)
    qi = np.arange(128)[None, :].astype(np.float32)
    dist_prev = 128 + qi - kj
    dist_own = qi - kj
    mask_prev = np.where(dist_prev < WINDOW, 0.0, NEG).astype(np.float32)
    mask_own = np.where(dist_own >= 0, 0.0, NEG).astype(np.float32)
    biasp = np.zeros((HKV, 128, 2, 4, 128), np.float32)
    for g in range(HKV):
        for h in range(4):
            biasp[g, :, 0, h, :] = -sl[g, h] * dist_prev + mask_prev
            biasp[g, :, 1, h, :] = -sl[g, h] * dist_own + mask_own
    biasp0 = biasp[:, :, 0].copy()
    if first_chunk:
        biasp0[:] = NEG
    j = np.arange(128)[:, None].astype(np.float32)
    t = np.arange(4)[None, :].astype(np.float32)
    dc = t + 128 - j
    mc = np.where(dc < WINDOW, 0.0, NEG).astype(np.float32)
    biasc = np.zeros((HKV, 128, NSB, 4, 4), np.float32)
    for g in range(HKV):
        for h in range(4):
            biasc[g, :, :, h, :] = (-sl[g, h] * dc + mc)[:, None, :]
    bk = np.repeat(np.arange(NSB), 4)[:, None]
    tk = np.tile(np.arange(4), NSB)[:, None].astype(np.float32)
    biasn = np.zeros((HKV, NS, NSB, 4, 4), np.float32)
    for g in range(HKV):
        for h in range(4):
            for b in range(NSB):
                for tt in range(4):
                    d = tt - tk[:, 0]
                    ok = (bk[:, 0] == b) & (d >= 0)
                    biasn[g, :, b, h, tt] = np.where(ok, -sl[g, h] * d, NEG)
    return dict(biasp=biasp, biasp0=biasp0, biasc=biasc.reshape(HKV, 128, 4 * NS),
                biasn=biasn.reshape(HKV, NS, 4 * NS), identf=np.eye(128, dtype=np.float32))


def make_in_maps(cfg, inp):
    c = cfg
    cps = c.n_cores // c.batch
    L = c.NT * 128
    xp, xs = np.asarray(inp["x_prompt"]), np.asarray(inp["x_sample"])
    cp, csm = np.asarray(inp["c_prompt"]), np.asarray(inp["c_sample"])
    shared = dict(
        w_ada=np.ascontiguousarray(inp["w_ada"][0]), b_ada=np.ascontiguousarray(inp["b_ada"][0]),
        norm_g=np.ascontiguousarray(inp["norm_g"][0]), w_in=np.ascontiguousarray(inp["w_in"][0]),
        q_g=np.ascontiguousarray(inp["q_norm_g"][0]), k_g=np.ascontiguousarray(inp["k_norm_g"][0]),
        sinks=np.ascontiguousarray(inp["sinks"][0]), conv_w=np.ascontiguousarray(inp["conv_w"][0]),
        conv_b=np.ascontiguousarray(inp["conv_b"][0]), ln_g=np.ascontiguousarray(inp["ln_g"][0]),
        ln_b=np.ascontiguousarray(inp["ln_b"][0]), w_pw2=np.ascontiguousarray(inp["w_pw2"][0]),
        w_out=np.ascontiguousarray(inp["w_out"][0]))
    shared = {kk: np.asarray(v, dtype=np.float32) for kk, v in shared.items()}
    tabs = {True: const_tables(c, True), False: const_tables(c, False)}
    maps = []
    for core in range(c.n_cores):
        b, ch = core // cps, core % cps
        first = (ch == 0)
        x = np.zeros((c.TT, c.D), np.float32)
        if not first:
            x[0:128] = xp[b, ch * L - 128:ch * L]
        x[128:128 + L] = xp[b, ch * L:(ch + 1) * L]
        sb = slice(core * NSB, (core + 1) * NSB)
        x[128 + L:] = xs[sb].reshape(NS, c.D)
        csel = np.zeros((128, c.D), np.float32)
        csel[0:NS] = np.repeat(csm[sb], 4, axis=0)
        csel[64] = cp[b]
        m = dict(shared)
        m.update(tabs[first])
        m.update(x=x, csel=csel,
                 ck=np.ascontiguousarray(inp["cache_k_win"][0][sb], dtype=np.float32),
                 cv=np.ascontiguousarray(inp["cache_v_win"][0][sb], dtype=np.float32),
                 st=np.ascontiguousarray(inp["state_conv"][0][sb], dtype=np.float32),
                 flag=np.full((128, 1), 0.0 if first else 1.0, np.float32))
        maps.append(m)
    return maps


def assemble(cfg, res):
    c = cfg
    cps = c.n_cores // c.batch
    L = c.NT * 128
    r = [{kk: np.asarray(v) for kk, v in rr.items()} for rr in res]
    y_p = np.stack([np.concatenate([r[b * cps + ch]["y_p"] for ch in range(cps)], 0) for b in range(c.batch)], 0)
    y_s = np.concatenate([rr["y_s"].reshape(NSB, 4, c.D) for rr in r], 0)
    lastc = [b * cps + cps - 1 for b in range(c.batch)]
    k_p = np.stack([r[i]["k_p"].reshape(128, c.HKV, 128) for i in lastc], 0)[None]
    v_p = np.stack([r[i]["v_p"].reshape(128, c.HKV, 128) for i in lastc], 0)[None]
    conv_p = np.stack([r[i]["conv_p"] for i in lastc], 0)[None]
    k_s = np.concatenate([np.concatenate([rr["k_so"], rr["k_sn"].reshape(NSB, 4, c.KVW)], 1) for rr in r], 0)
    v_s = np.concatenate([np.concatenate([rr["v_so"], rr["v_sn"].reshape(NSB, 4, c.KVW)], 1) for rr in r], 0)
    conv_s = np.concatenate([np.concatenate([rr["conv_so"], rr["conv_sn"].reshape(NSB, 4, c.DC)], 1) for rr in r], 0)
    k_s = k_s.reshape(-1, WINDOW, c.HKV, 128)[None]
    v_s = v_s.reshape(-1, WINDOW, c.HKV, 128)[None]
    conv_s = conv_s[None]
    f = lambda a: np.ascontiguousarray(a, dtype=np.float32)
    return (f(y_p), f(y_s), f(k_p), f(v_p), f(conv_p), f(k_s), f(v_s), f(conv_s))


_NC_CACHE = {}


def kernel(**inputs):
    cfg = Cfg()
    if "nc" not in _NC_CACHE:
        _NC_CACHE["nc"] = build(cfg)
    nc = _NC_CACHE["nc"]
    maps = make_in_maps(cfg, inputs)
    res = run_bass_kernel_spmd(nc, maps, core_ids=list(range(cfg.n_cores)))
    return assemble(cfg, res.results)
```
